# Optimizing a Trainium2 kernel written in Bass

```python
import jax, jax.numpy as jnp
from jax import lax
import numpy as np

D_MODEL = 1024
BATCH = 4
SEQ = 8192
DEPTH = 4

CTX_LEN = 256
GRID_W = 64
HEAD_DIM = 64
ROPE_FREQS = HEAD_DIM // 4
ROPE_THETA = 10000.0
NORM_EPS = 1e-6
NEG_INF = -1e30
ATTN_SCALE = HEAD_DIM ** -0.5
Q_BLOCK = 128

POOL_WIDTH = D_MODEL // 4
POOL_WINDOWS = (2, 4, 8, 16)
POOL_GROUP = POOL_WIDTH // len(POOL_WINDOWS)

WIN_HEADS = (3 * D_MODEL // 8) // HEAD_DIM
WIN_KV_HEADS = 2
WIN_GROUP = WIN_HEADS // WIN_KV_HEADS
WINDOW = 128

GLB_HEADS = (3 * D_MODEL // 8) // HEAD_DIM
GLB_KV_HEADS = 2
GLB_GROUP = GLB_HEADS // GLB_KV_HEADS

WIN_WIDTH = WIN_HEADS * HEAD_DIM
WIN_KV_WIDTH = WIN_KV_HEADS * HEAD_DIM
GLB_WIDTH = GLB_HEADS * HEAD_DIM
GLB_KV_WIDTH = GLB_KV_HEADS * HEAD_DIM
MIX_WIDTH = POOL_WIDTH + WIN_WIDTH + GLB_WIDTH
IN_WIDTH = POOL_WIDTH + WIN_WIDTH + 2 * WIN_KV_WIDTH + GLB_WIDTH + 2 * GLB_KV_WIDTH

D_FF = -(-8 * D_MODEL // (3 * 256)) * 256
N_MOD = 6

kernel_name = "hybrid_pool_window_global_dit_block"


def rms_norm(x, g):
    x32 = x.astype(jnp.float32)
    y = x32 * lax.rsqrt(jnp.mean(x32 * x32, axis=-1, keepdims=True) + NORM_EPS)
    return (y * g.astype(jnp.float32)).astype(x.dtype)


def modulate(h, shift, scale):
    return h * (1.0 + scale) + shift


def rope_tables(n):
    rows = n // GRID_W
    row = jnp.broadcast_to(jnp.arange(rows)[:, None], (rows, GRID_W)).reshape(-1).astype(jnp.float32)
    col = jnp.broadcast_to(jnp.arange(GRID_W)[None, :], (rows, GRID_W)).reshape(-1).astype(jnp.float32)
    freq = ROPE_THETA ** (-jnp.arange(ROPE_FREQS, dtype=jnp.float32) / ROPE_FREQS)
    ang = jnp.stack([row[:, None] * freq, col[:, None] * freq], axis=1)
    ang = ang[:, None, :, None, :]
    return jnp.cos(ang), jnp.sin(ang)


def apply_rope2d(x, cos, sin):
    b, n, h, d = x.shape
    xr = x.reshape(b, n, h, 2, 2, d // 4)
    rot = jnp.stack([-xr[..., 1, :], xr[..., 0, :]], axis=-2)
    return (xr * cos + rot * sin).reshape(b, n, h, d).astype(x.dtype)


def attend(q, keys, values, mask=None, sink=None):
    logits = [jnp.einsum('bqkgd,bckd->bkgqc', q, k).astype(jnp.float32) * ATTN_SCALE for k in keys]
    if mask is not None:
        logits[0] = jnp.where(mask, logits[0], NEG_INF)
    if sink is not None:
        b, nq, kvh, g, _ = q.shape
        logits.append(jnp.broadcast_to(sink.astype(jnp.float32)[None, :, :, None, None], (b, kvh, g, nq, 1)))
    p = jax.nn.softmax(jnp.concatenate(logits, axis=-1), axis=-1)
    out = None
    off = 0
    for v in values:
        size = v.shape[1]
        term = jnp.einsum('bkgqc,bckd->bqkgd', p[..., off:off + size].astype(v.dtype), v)
        out = term if out is None else out + term
        off += size
    return out


def merge_blocks(o):
    nb, b, nq = o.shape[:3]
    return jnp.transpose(o, (1, 0, 2, 3, 4, 5)).reshape(b, nb * nq, -1)


def pool_mix(a, w_pool_l, pool_scale_l):
    b, n, _ = a.shape
    a32 = a.astype(jnp.float32)
    cs = jnp.concatenate([jnp.zeros((b, 1, POOL_WIDTH), jnp.float32), jnp.cumsum(a32, axis=1)], axis=1)
    t = jnp.arange(n)
    feats = []
    for g, w in enumerate(POOL_WINDOWS):
        lo = jnp.clip(t - w // 2, 0, n - 1)
        hi = jnp.clip(t + (w - 1 - w // 2), 0, n - 1)
        cnt = (hi - lo + 1).astype(jnp.float32)[None, :, None]
        sl = slice(g * POOL_GROUP, (g + 1) * POOL_GROUP)
        csg = cs[..., sl]
        feats.append((csg[:, hi + 1] - csg[:, lo]) / cnt - a32[..., sl])
    p = jnp.stack(feats, axis=2)
    y = jnp.einsum('bngc,gcd->bngd', p, w_pool_l.astype(jnp.float32)).reshape(b, n, POOL_WIDTH)
    return (y * pool_scale_l.astype(jnp.float32)).astype(a.dtype)


def window_attention(q, k, v, kc, vc, sink):
    n = q.shape[1]
    pad = ((0, 0), (WINDOW, WINDOW), (0, 0), (0, 0))
    kp = jnp.pad(k, pad)
    vp = jnp.pad(v, pad)
    span = Q_BLOCK + 2 * WINDOW

    def block(i):
        start = i * Q_BLOCK
        qi = lax.dynamic_slice_in_dim(q, start, Q_BLOCK, axis=1)
        ki = lax.dynamic_slice_in_dim(kp, start, span, axis=1)
        vi = lax.dynamic_slice_in_dim(vp, start, span, axis=1)
        qpos = start + jnp.arange(Q_BLOCK)
        kpos = start - WINDOW + jnp.arange(span)
        mask = (jnp.abs(kpos[None, :] - qpos[:, None]) <= WINDOW) & (kpos >= 0)[None, :] & (kpos < n)[None, :]
        return attend(qi, [ki, kc], [vi, vc], mask=mask, sink=sink)

    return merge_blocks(lax.map(block, jnp.arange(n // Q_BLOCK)))


def global_attention(q, k, v, kc, vc):
    n = q.shape[1]

    def block(i):
        qi = lax.dynamic_slice_in_dim(q, i * Q_BLOCK, Q_BLOCK, axis=1)
        return attend(qi, [k, kc], [v, vc])

    return merge_blocks(lax.map(block, jnp.arange(n // Q_BLOCK)))


def mixer_inputs(h, w_in_l, g_qn, g_kn, rope):
    u = h @ w_in_l
    b, n, _ = u.shape
    widths = [POOL_WIDTH, WIN_WIDTH, WIN_KV_WIDTH, WIN_KV_WIDTH, GLB_WIDTH, GLB_KV_WIDTH, GLB_KV_WIDTH]
    offs = [int(o) for o in np.cumsum(widths)[:-1]]
    a, qw, kw, vw, qg, kg, vg = jnp.split(u, offs, axis=-1)
    qw = qw.reshape(b, n, WIN_HEADS, HEAD_DIM)
    kw = kw.reshape(b, n, WIN_KV_HEADS, HEAD_DIM)
    vw = vw.reshape(b, n, WIN_KV_HEADS, HEAD_DIM)
    qg = rms_norm(qg.reshape(b, n, GLB_HEADS, HEAD_DIM), g_qn)
    kg = rms_norm(kg.reshape(b, n, GLB_KV_HEADS, HEAD_DIM), g_kn)
    vg = vg.reshape(b, n, GLB_KV_HEADS, HEAD_DIM)
    if rope is not None:
        cos, sin = rope
        qw = apply_rope2d(qw, cos, sin)
        kw = apply_rope2d(kw, cos, sin)
        qg = apply_rope2d(qg, cos, sin)
        kg = apply_rope2d(kg, cos, sin)
    qw = qw.reshape(b, n, WIN_KV_HEADS, WIN_GROUP, HEAD_DIM)
    qg = qg.reshape(b, n, GLB_KV_HEADS, GLB_GROUP, HEAD_DIM)
    return a, qw, kw, vw, qg, kg, vg


def swiglu_sublayer(x, shift, scale, gate, g_pre, g_post, w_gate_l, w_up_l, w_down_l):
    h = modulate(rms_norm(x, g_pre), shift, scale)
    f = (jax.nn.silu(h @ w_gate_l) * (h @ w_up_l)) @ w_down_l
    return x + gate * rms_norm(f, g_post)


def setup_inputs(seed: int = 0) -> dict:
    key = jax.random.key(seed)
    ks = jax.random.split(key, 20)
    f32 = jnp.float32

    def nrm(k, shape, scale):
        return jax.random.normal(k, shape, f32) * scale

    def gain(k, shape):
        return 1.0 + 0.05 * jax.random.normal(k, shape, f32)

    return {
        "x": nrm(ks[0], (BATCH, SEQ, D_MODEL), 1.0),
        "c": nrm(ks[1], (BATCH, D_MODEL), 1.0),
        "ctx": nrm(ks[2], (BATCH, CTX_LEN, D_MODEL), 1.0),
        "c_ctx": nrm(ks[3], (D_MODEL,), 1.0),
        "w_mod": nrm(ks[4], (DEPTH, D_MODEL, N_MOD * D_MODEL), 0.5 * D_MODEL ** -0.5),
        "b_mod": nrm(ks[5], (DEPTH, N_MOD * D_MODEL), 0.01),
        "g_pre_mix": gain(ks[6], (DEPTH, D_MODEL)),
        "g_post_mix": gain(ks[7], (DEPTH, D_MODEL)),
        "g_pre_ffn": gain(ks[8], (DEPTH, D_MODEL)),
        "g_post_ffn": gain(ks[9], (DEPTH, D_MODEL)),
        "w_in": nrm(ks[10], (DEPTH, D_MODEL, IN_WIDTH), D_MODEL ** -0.5),
        "w_pool": nrm(ks[11], (DEPTH, len(POOL_WINDOWS), POOL_GROUP, POOL_GROUP), POOL_GROUP ** -0.5),
        "pool_scale": gain(ks[12], (DEPTH, POOL_WIDTH)),
        "win_sink": nrm(ks[13], (DEPTH, WIN_HEADS), 0.5),
        "g_qnorm": gain(ks[14], (DEPTH, HEAD_DIM)),
        "g_knorm": gain(ks[15], (DEPTH, HEAD_DIM)),
        "w_out": nrm(ks[16], (DEPTH, MIX_WIDTH, D_MODEL), MIX_WIDTH ** -0.5),
        "w_gate": nrm(ks[17], (DEPTH, D_MODEL, D_FF), D_MODEL ** -0.5),
        "w_up": nrm(ks[18], (DEPTH, D_MODEL, D_FF), D_MODEL ** -0.5),
        "w_down": nrm(ks[19], (DEPTH, D_FF, D_MODEL), D_FF ** -0.5),
    }


def reference(x, c, ctx, c_ctx, w_mod, b_mod, g_pre_mix, g_post_mix, g_pre_ffn, g_post_ffn,
              w_in, w_pool, pool_scale, win_sink, g_qnorm, g_knorm, w_out, w_gate, w_up, w_down):
    b, n, d = x.shape
    rope = rope_tables(n)
    silu_c = jax.nn.silu(c)
    silu_cc = jax.nn.silu(c_ctx)
    xc = ctx
    for l in range(DEPTH):
        last = l == DEPTH - 1
        m = (silu_c @ w_mod[l] + b_mod[l]).reshape(b, N_MOD, 1, d)
        mc = (silu_cc @ w_mod[l] + b_mod[l]).reshape(N_MOD, 1, 1, d)
        sink = win_sink[l].reshape(WIN_KV_HEADS, WIN_GROUP)

        h = modulate(rms_norm(x, g_pre_mix[l]), m[:, 0], m[:, 1])
        hc = modulate(rms_norm(xc, g_pre_mix[l]), mc[0], mc[1])
        a, qw, kw, vw, qg, kg, vg = mixer_inputs(h, w_in[l], g_qnorm[l], g_knorm[l], rope)
        ac, qwc, kwc, vwc, qgc, kgc, vgc = mixer_inputs(hc, w_in[l], g_qnorm[l], g_knorm[l], None)

        y = jnp.concatenate([
            pool_mix(a, w_pool[l], pool_scale[l]),
            window_attention(qw, kw, vw, kwc, vwc, sink),
            global_attention(qg, kg, vg, kgc, vgc),
        ], axis=-1) @ w_out[l]
        x_new = x + m[:, 2] * rms_norm(y, g_post_mix[l])

        if not last:
            lc = xc.shape[1]
            yc = jnp.concatenate([
                pool_mix(ac, w_pool[l], pool_scale[l]),
                attend(qwc, [kwc], [vwc], sink=sink).reshape(b, lc, -1),
                attend(qgc, [kgc], [vgc]).reshape(b, lc, -1),
            ], axis=-1) @ w_out[l]
            xc = xc + mc[2] * rms_norm(yc, g_post_mix[l])
            xc = swiglu_sublayer(xc, mc[3], mc[4], mc[5], g_pre_ffn[l], g_post_ffn[l],
                                 w_gate[l], w_up[l], w_down[l])

        x = swiglu_sublayer(x_new, m[:, 3], m[:, 4], m[:, 5], g_pre_ffn[l], g_post_ffn[l],
                            w_gate[l], w_up[l], w_down[l])
    return x
```

```python
from contextlib import ExitStack
import numpy as np
import ml_dtypes
import concourse.bass as bass
import concourse.mybir as mybir
from concourse.bass_utils import run_bass_kernel_spmd

F32 = mybir.dt.float32
BF16 = mybir.dt.bfloat16
AF = mybir.ActivationFunctionType
ALU = mybir.AluOpType

D = 1024
SEQ = 8192
BATCH = 4
DEPTH = 4
CTX = 256
HALF = SEQ // 2
T = 512
NCH = HALF // T
DFF = 2816
NFF = DFF // 128
EPS = 1e-6
NMOD = 6


class Buf:
    __slots__ = ("name", "wtok", "rtoks", "dsem", "dcount")

    def __init__(self, name):
        self.name = name
        self.wtok = None
        self.rtoks = {}
        self.dsem = None
        self.dcount = 0


class Prog:
    def __init__(self, nc, stack):
        self.nc = nc
        self.stack = stack
        self.engs = {"pe": nc.tensor, "act": nc.scalar, "dve": nc.vector, "pool": nc.gpsimd, "sp": nc.sync}
        self.sems = {}
        self.count = {}
        for e in self.engs:
            self.sems[e] = stack.enter_context(nc.semaphore("s_" + e))
            self.count[e] = 0
        self.waited = {e: {} for e in self.engs}
        self.ninstr = 0

    def _need(self, eng, tok):
        if tok is None:
            return
        key, val = tok
        if eng == "pe" and key == "pe":
            return
        w = self.waited[eng]
        if w.get(key, 0) >= val:
            return
        w[key] = val
        self.engs[eng].wait_ge(self.sems[key], val)

    def _deps(self, eng, reads, writes, skip_key=None):
        for b in reads:
            self._need(eng, b.wtok)
        for b in writes:
            if b.wtok is not None and b.wtok[0] != skip_key:
                self._need(eng, b.wtok)
            for k, v in b.rtoks.items():
                self._need(eng, (k, v))

    @staticmethod
    def _commit(tok, reads, writes):
        k, v = tok
        for b in reads:
            if b.rtoks.get(k, 0) < v:
                b.rtoks[k] = v
        for b in writes:
            b.wtok = tok
            b.rtoks = {}

    def op(self, eng, fn, reads=(), writes=(), inc=True):
        self._deps(eng, reads, writes)
        ins = fn()
        self.ninstr += 1
        if inc:
            self.count[eng] += 1
            ins.then_inc(self.sems[eng], 1)
            tok = (eng, self.count[eng])
        else:
            tok = (eng, self.count[eng] + 1)
        self._commit(tok, reads, writes)
        return tok

    def dma(self, eng, out, in_, reads=(), writes=(), **kw):
        assert len(writes) == 1
        wb = writes[0]
        if wb.dsem is None:
            key = "d_" + wb.name
            self.sems[key] = self.stack.enter_context(self.nc.semaphore(key))
            wb.dsem = key
        self._deps(eng, reads, writes, skip_key=wb.dsem)
        wb.dcount += 16
        ins = self.engs[eng].dma_start(out=out, in_=in_, **kw)
        ins.then_inc(self.sems[wb.dsem], 16)
        self.ninstr += 1
        tok = (wb.dsem, wb.dcount)
        rt = wb.rtoks
        self._commit(tok, reads, writes)
        return tok

    def wait_bufs(self, eng, bufs):
        for b in bufs:
            if b.wtok is not None:
                key, val = b.wtok
                w = self.waited[eng]
                if w.get(key, 0) < val:
                    w[key] = val
                    self.engs[eng].wait_ge(self.sems[key], val)


RATIO = 3


def build_program(n_layers=DEPTH, final_layers=None):
    nc = bass.Bass("TRN2", target_bir_lowering=False, num_devices=8)
    L = n_layers
    FL = L if final_layers is None else final_layers

    def din(name, shape, dt=F32):
        return nc.dram_tensor(name, list(shape), dt, kind="ExternalInput").ap()

    xT_in = din("xT", [8, 128, HALF])
    xcT_in = din("xcT", [128, 8, CTX])
    ccT_in = din("ccT", [128, 8, 2])
    wmod_in = din("wmod", [L, 12, 128, 8, 512])
    bmod_in = din("bmod", [L, 2, 6144])
    gains_in = din("gains", [128, L, 4, 8])
    pscale_in = din("pscale", [128, L, 2])
    sink_in = din("sink", [128, L, 3])
    gqk_in = din("gqk", [128, L, 2])
    win_in = din("w_in", [L, 12, 128, 8, 128])
    wout_in = din("w_out", [L, 8, 128, 8, 128])
    wgate_in = din("w_gate", [L, NFF, 128, 8, 128])
    wup_in = din("w_up", [L, NFF, 128, 8, 128])
    wdown_in = din("w_down", [L, 8, 128, NFF, 128])
    wpool_in = din("w_pool", [L, 128, 2, 128])
    ropeC_in = din("ropeC", [128, HALF])
    ropeS_in = din("ropeS", [128, HALF])
    consts_in = din("consts", [128, 3, 128])
    cbf_in = din("cbf", [128, 2, 128], BF16)
    masks_in = din("masks", [128, 8, T], BF16)
    icnt_in = din("icnt", [128, 3, 2, 16])
    halo_in = din("halo", [128, 2])
    out_T = nc.dram_tensor("outT", [8, 128, HALF], F32, kind="ExternalOutput").ap()

    def dint(name, shape, dt, shared=False):
        if shared:
            return nc.dram_tensor(name, list(shape), dt, addr_space="Shared").ap()
        return nc.dram_tensor(name, list(shape), dt).ap()

    x_s = dint("x_s", [8, 128, HALF], F32)
    win_s = dint("win_s", [L, 12, 128, 8, 128], BF16)
    wout_s = dint("wout_s", [L, 8, 128, 8, 128], BF16)
    wgate_s = dint("wgate_s", [L, NFF, 128, 8, 128], BF16)
    wup_s = dint("wup_s", [L, NFF, 128, 8, 128], BF16)
    wdown_s = dint("wdown_s", [L, 8, 128, NFF, 128], BF16)
    qw_s = dint("qw_s", [2, 128, 3, HALF], BF16)
    qg_s = dint("qg_s", [2, 128, 3, HALF], BF16)
    kw_t = [[dint(f"kw_sh{p}_{c}", [2, 128, T], BF16, True) for c in range(NCH)] for p in range(2)]
    kg_t = [[dint(f"kg_sh{p}_{c}", [2, 128, T], BF16, True) for c in range(NCH)] for p in range(2)]
    vw_t = [[dint(f"vw_sh{p}_{c}", [2, 4, 128, 256], BF16, True) for c in range(NCH)] for p in range(2)]
    vg_t = [[dint(f"vg_sh{p}_{c}", [2, 4, 128, 256], BF16, True) for c in range(NCH)] for p in range(2)]
    a_t = [[dint(f"a_sh{p}_{c}", [2, 128, 2, T], F32, True) for c in range(NCH)] for p in range(2)]

    st = ExitStack()
    with st:
        P = Prog(nc, st)
        bufs = {}

        def B(name):
            if name not in bufs:
                bufs[name] = Buf(name)
            return bufs[name]

        def sb(name, shape, dt):
            return st.enter_context(nc.sbuf_tensor("sb_" + name, list(shape), dt))

        def psum(name, shape):
            return st.enter_context(nc.psum_tensor(name, list(shape), F32))

        par = nc.sync.partition_id() % 2
        par_p = nc.gpsimd.partition_id() % 2

        consts = sb("consts", [128, 3, 128], F32)
        cbf = sb("cbf", [128, 2, 128], BF16)
        masks = sb("masks", [128, 8, T], BF16)
        icnt = sb("icnt", [128, 3, 2, 16], F32)
        halo = sb("halo", [128, 2], F32)
        gains = sb("gains", [128, L, 4, 8], F32)
        pscale = sb("pscale", [128, L, 2], F32)
        esink = sb("esink", [128, L, 3], F32)
        gqk = sb("gqk", [128, L, 2], F32)
        epsc = sb("epsc", [128, 1], F32)
        modT = sb("modT", [128, L, 48, 2], F32)
        tabs = sb("tabs", [128, L, 6, 8, 2], F32)
        ccT = sb("ccT", [128, 8, 2], F32)
        scc = sb("scc", [128, 8, 2], F32)
        sccb = sb("sccb", [128, 8, 2], BF16)
        xc = sb("xc", [128, 8, CTX], F32)
        wpool = sb("wpool", [128, 2, 128], BF16)
        kwc = sb("kwc", [128, CTX], BF16)
        kgc = sb("kgc", [128, CTX], BF16)
        vwc = sb("vwc", [128, 2, 2, 128], BF16)
        vgc = sb("vgc", [128, 2, 2, 128], BF16)
        qwc = sb("qwc", [128, 3, CTX], BF16)
        qgc = sb("qgc", [128, 3, CTX], BF16)
        aext = sb("aext", [128, 2, T + 16], F32)
        xa = sb("xa", [128, 8, T], F32)
        hT = sb("hT", [128, 8, T], BF16)
        rstd = sb("rstd", [128, T], F32)
        tmp = [sb(f"tmp{i}", [128, T], F32) for i in range(4)]
        qf = [sb(f"qf{i}", [128, T], F32) for i in range(2)]
        sqh = sb("sqh", [128, T], BF16)
        ropeC = sb("ropeC", [128, T], F32)
        ropeS = sb("ropeS", [128, T], F32)
        ringA = [sb(f"ringA{i}", [128, 8, 128], BF16) for i in range(8)]
        ringB = [sb(f"ringB{i}", [128, NFF, 128], BF16) for i in range(2)]
        qst = [sb(f"qst{i}", [128, 3, T], BF16) for i in range(2)]
        kst = [sb(f"kst{i}", [128, T], BF16) for i in range(2)]
        vst = [sb(f"vst{i}", [128, 4, 2, 128], BF16) for i in range(2)]
        ast = sb("ast", [128, 2, T], F32)
        kwin = sb("kwin", [128, 6 * 128], BF16)
        vwin = sb("vwin", [128, 6, 2, 128], BF16)
        kgs = [sb(f"kgs{i}", [128, 1024], BF16) for i in range(2)]
        vgs = [sb(f"vgs{i}", [128, 8, 2, 128], BF16) for i in range(2)]
        PT = [sb(f"PT{i}", [128, 2 * T], BF16) for i in range(3)]
        dsb = sb("dsb", [128, T], F32)
        rec = sb("rec", [128, T], F32)
        yT = sb("yT", [128, 8, T], BF16)
        yo = sb("yo", [128, 8, T], F32)
        actT = sb("actT", [128, NFF, T], BF16)
        sg = [sb(f"sg{i}", [128, T], F32) for i in range(2)]
        feat = sb("feat", [128, 2, T], BF16)
        sq = actT[:, 0:8, :]
        yo_flat = yo[:].rearrange("p k t -> p (k t)")
        pwt = [sb(f"pwt{i}", [128, T + 16], F32) for i in range(3)]
        mrow = yo_flat[0:2, 0:2048]
        brow = yo_flat[0:2, 2048:4096]

        ps_s = [psum(f"ps_s{i}", [128, 2 * T]) for i in range(2)]
        ps_acc = [psum(f"ps_acc{i}", [128, T]) for i in range(2)]
        ps_m = [psum(f"ps_m{i}", [128, T]) for i in range(2)]

        def mm(out, lhsT, rhs, start, stop, reads, writes, inc):
            return P.op("pe", lambda: nc.tensor.matmul(out, lhsT, rhs, start=start, stop=stop),
                        reads=reads, writes=writes, inc=inc)

        def act(out, in_, func, reads, writes, bias=None, scale=None):
            kw = {}
            if bias is not None:
                kw["bias"] = bias
            if scale is not None:
                kw["scale"] = scale
            return P.op("act", lambda: nc.scalar.activation(out=out, in_=in_, func=func, **kw), reads=reads, writes=writes)

        def tt(out, in0, in1, op, reads, writes):
            return P.op("dve", lambda: nc.vector.tensor_tensor(out, in0, in1, op), reads=reads, writes=writes)

        def stt(out, in0, scalar, in1, op0, op1, reads, writes):
            return P.op("dve", lambda: nc.vector.scalar_tensor_tensor(out, in0, scalar, in1, op0, op1), reads=reads, writes=writes)

        def tsm(out, in0, s1, reads, writes, op=ALU.mult):
            return P.op("dve", lambda: nc.vector.tensor_scalar(out, in0, s1, None, op), reads=reads, writes=writes)

        def recip(out, in_, reads, writes):
            return P.op("dve", lambda: nc.vector.reciprocal(out, in_), reads=reads, writes=writes)

        def vcopy(out, in_, reads, writes):
            return P.op("dve", lambda: nc.vector.tensor_copy(out, in_), reads=reads, writes=writes)

        def load(out, in_, reads, writes):
            return P.dma("sp", out, in_, reads=reads, writes=writes)

        def store(out, in_, reads, writes):
            return P.dma("pool", out, in_, reads=reads, writes=writes)

        load(consts[:], consts_in, [], [B("consts")])
        load(cbf[:], cbf_in, [], [B("cbf")])
        load(masks[:], masks_in, [], [B("masks")])
        load(icnt[:], icnt_in, [], [B("icnt")])
        load(halo[:], halo_in, [], [B("halo")])
        load(gains[:], gains_in, [], [B("gains")])
        load(pscale[:], pscale_in, [], [B("pscale")])
        load(esink[:], sink_in, [], [B("esink")])
        load(gqk[:], gqk_in, [], [B("gqk")])
        load(ccT[:], ccT_in, [], [B("ccT")])
        load(xc[:], xcT_in, [], [B("xc")])
        P.op("dve", lambda: nc.vector.memset(epsc[:], EPS), writes=[B("epsc")])
        for i in range(2):
            P.op("dve", lambda i=i: nc.vector.memset(vst[i][:], 1.0), writes=[B(f"vst{i}")])
        P.op("dve", lambda: nc.vector.memset(vwc[:], 1.0), writes=[B("vwc")])
        P.op("dve", lambda: nc.vector.memset(vgc[:], 1.0), writes=[B("vgc")])
        act(esink[:], esink[:], AF.Exp, [B("esink")], [B("esink")])
        act(scc[:], ccT[:], AF.Exp, [B("ccT")], [B("scc")], scale=-1.0)
        tsm(scc[:], scc[:], 1.0, [B("scc")], [B("scc")], op=ALU.add)
        recip(scc[:], scc[:], [B("scc")], [B("scc")])
        tt(scc[:], ccT[:], scc[:], ALU.mult, [B("ccT"), B("scc")], [B("scc")])
        vcopy(sccb[:], scc[:], [B("scc")], [B("sccb")])
        perm = consts[:, 0, :]
        swap = consts[:, 1, :]
        eye2 = consts[0:2, 2, 0:2]
        ones_bf = cbf[:, 0, :]
        bd_bf = cbf[:, 1, :]

        def cast_layer(l):
            def cast(dst, src, name, nsplit):
                n0 = dst.shape[0]
                step = max(1, n0 // nsplit)
                for i in range(0, n0, step):
                    P.dma("pool", dst[i:i + step], src[i:i + step], reads=[], writes=[B(name)])
            cast(win_s[l], win_in[l], f"win_s{l}", 4)
            cast(wout_s[l], wout_in[l], f"wout_s{l}", 2)
            cast(wgate_s[l], wgate_in[l], f"wgate_s{l}", 11)
            cast(wup_s[l], wup_in[l], f"wup_s{l}", 11)
            cast(wdown_s[l], wdown_in[l], f"wdown_s{l}", 8)

        cast_layer(0)

        for l in range(L):
            for g3 in range(3):
                load(brow, bmod_in[l, :, g3 * 2048:(g3 + 1) * 2048], [], [B("yo")])
                for n4 in range(4):
                    n = g3 * 4 + n4
                    load(xa[:], wmod_in[l, n], [], [B("xa")])
                    for k in range(8):
                        mm(ps_m[0][0:2, :], scc[:, k, :], xa[:, k, :], k == 0, k == 7,
                           [B("scc"), B("xa")], [B("ps_m0")], k == 7)
                    tt(mrow[:, n4 * 512:(n4 + 1) * 512], ps_m[0][0:2, :], brow[:, n4 * 512:(n4 + 1) * 512], ALU.add,
                       [B("ps_m0"), B("yo")], [B("yo")])
                for jj in range(16):
                    j = g3 * 16 + jj
                    mm(ps_m[1][:, 2 * j:2 * j + 2], mrow[0:2, jj * 128:(jj + 1) * 128], eye2, True, True,
                       [B("yo"), B("consts")], [B("ps_m1")], True)
            vcopy(modT[:, l, :, :], ps_m[1][:, 0:96].rearrange("p (j v) -> p j v", v=2), [B("ps_m1")], [B("modT")])
            for v in range(2):
                for s, (gi_pre, gi_post) in enumerate(((0, 1), (2, 3))):
                    m0 = 3 * s
                    stt(tabs[:, l, 3 * s + 0, :, v], modT[:, l, (m0 + 1) * 8:(m0 + 2) * 8, v], 1.0, gains[:, l, gi_pre, :],
                        ALU.add, ALU.mult, [B("modT"), B("gains")], [B("tabs")])
                    vcopy(tabs[:, l, 3 * s + 1, :, v], modT[:, l, m0 * 8:(m0 + 1) * 8, v], [B("modT")], [B("tabs")])
                    tt(tabs[:, l, 3 * s + 2, :, v], modT[:, l, (m0 + 2) * 8:(m0 + 3) * 8, v], gains[:, l, gi_post, :], ALU.mult,
                       [B("modT"), B("gains")], [B("tabs")])

        def rms_stats(src_ap, src_bufs, nt, width):
            act(sq[:, :, 0:nt], src_ap, AF.Square, src_bufs, [B("actT")])
            for k in range(8):
                mm(ps_m[0][:, 0:nt], ones_bf, sq[:, k, 0:nt], k == 0, k == 7, [B("cbf"), B("actT")], [B("ps_m0")], k == 7)
            act(rstd[:, 0:nt], ps_m[0][:, 0:nt], AF.Ln, [B("ps_m0"), B("epsc")], [B("rstd")], bias=epsc[:, 0:1], scale=1.0 / width)
            act(rstd[:, 0:nt], rstd[:, 0:nt], AF.Exp, [B("rstd")], [B("rstd")], scale=-0.5)

        def norm_mod(src, src_bufs, nt, l, ti, v):
            rms_stats(src[:, :, 0:nt], src_bufs, nt, D)
            for k in range(8):
                t = tmp[k % 2]
                stt(t[:, 0:nt], src[:, k, 0:nt], tabs[:, l, ti, k, v:v + 1], rstd[:, 0:nt], ALU.mult, ALU.mult,
                    src_bufs + [B("tabs"), B("rstd")], [B(f"tmp{k % 2}")])
                act(hT[:, k, 0:nt], t[:, 0:nt], AF.Identity, [B(f"tmp{k % 2}"), B("tabs")], [B("hT")],
                    bias=tabs[:, l, ti + 1, k, v:v + 1], scale=1.0)

        def post_res(nt, l, gi, v, xbuf, xname):
            rms_stats(yo[:, :, 0:nt], [B("yo")], nt, D)
            for k in range(8):
                t = tmp[k % 2]
                tt(t[:, 0:nt], yo[:, k, 0:nt], rstd[:, 0:nt], ALU.mult, [B("yo"), B("rstd")], [B(f"tmp{k % 2}")])
                stt(xbuf[:, k, 0:nt], t[:, 0:nt], tabs[:, l, gi, k, v:v + 1], xbuf[:, k, 0:nt], ALU.mult, ALU.add,
                    [B(f"tmp{k % 2}"), B("tabs"), B(xname)], [B(xname)])

        ra = [0]
        rb = [0]

        def ldA(src_ap, srcb):
            i = ra[0] % 8
            ra[0] += 1
            load(ringA[i][:], src_ap, [srcb], [B(f"ringA{i}")])
            return ringA[i], B(f"ringA{i}")

        def ldB(src_ap, srcb):
            i = rb[0] % 2
            rb[0] += 1
            load(ringB[i][:], src_ap, [srcb], [B(f"ringB{i}")])
            return ringB[i], B(f"ringB{i}")

        class Stream:
            def __init__(self, seq, ahead=7):
                self.seq = seq
                self.loaded = []
                self.ahead = ahead

            def get(self, i):
                while len(self.loaded) < min(len(self.seq), i + self.ahead):
                    self.loaded.append(ldA(*self.seq[len(self.loaded)]))
                return self.loaded[i]

        def qk_post(psb, psname, nt, l, is_glb, gcol, rope, dst, dst_bufs, slot):
            f = qf[slot]
            fb = B(f"qf{slot}")
            if is_glb:
                act(sqh[:, 0:nt], psb[:, 0:nt], AF.Square, [B(psname)], [B("sqh")])
                mm(ps_m[0][:, 0:nt], bd_bf, sqh[:, 0:nt], True, True, [B("cbf"), B("sqh")], [B("ps_m0")], True)
                act(rstd[:, 0:nt], ps_m[0][:, 0:nt], AF.Ln, [B("ps_m0"), B("epsc")], [B("rstd")], bias=epsc[:, 0:1], scale=1.0 / 64)
                act(rstd[:, 0:nt], rstd[:, 0:nt], AF.Exp, [B("rstd")], [B("rstd")], scale=-0.5)
                stt(f[:, 0:nt] if rope else dst, psb[:, 0:nt], gqk[:, l, gcol:gcol + 1], rstd[:, 0:nt], ALU.mult, ALU.mult,
                    [B(psname), B("gqk"), B("rstd")], [fb] if rope else dst_bufs)
            else:
                act(f[:, 0:nt] if rope else dst, psb[:, 0:nt], AF.Copy, [B(psname)], [fb] if rope else dst_bufs)
            if rope:
                mm(ps_m[1][:, 0:nt], perm, f[:, 0:nt], True, True, [B("consts"), fb], [B("ps_m1")], True)
                tt(tmp[2][:, 0:nt], f[:, 0:nt], ropeC[:, 0:nt], ALU.mult, [fb, B("ropeC")], [B("tmp2")])
                tt(tmp[3][:, 0:nt], ps_m[1][:, 0:nt], ropeS[:, 0:nt], ALU.mult, [B("ps_m1"), B("ropeS")], [B("tmp3")])
                tt(dst, tmp[2][:, 0:nt], tmp[3][:, 0:nt], ALU.add, [B("tmp2"), B("tmp3")], dst_bufs)

        PSROT = [(ps_s[0], "ps_s0", 0), (ps_s[0], "ps_s0", 1), (ps_s[1], "ps_s1", 0), (ps_s[1], "ps_s1", 1),
                 (ps_acc[0], "ps_acc0", 0), (ps_acc[1], "ps_acc1", 0)]

        def in_proj(nt, l, latent, dsts, need_q=True):
            ws = Stream([(win_s[l, mc], B(f"win_s{l}")) for mc in range(12)])
            rot = [0]

            def proj_fm(mc):
                t_, name, hf = PSROT[rot[0] % len(PSROT)]
                rot[0] += 1
                o = t_[:, hf * T:hf * T + nt]
                w_, wb = ws.get(mc)
                for k in range(8):
                    mm(o, w_[:, k, :], hT[:, k, 0:nt], k == 0, k == 7, [wb, B("hT")], [B(name + f"_{hf}")], k == 7)
                return t_[:, hf * T:(hf + 1) * T], name + f"_{hf}"

            def proj_v(mc, key):
                t_, name, hf = PSROT[rot[0] % len(PSROT)]
                rot[0] += 1
                w_, wb = ws.get(mc)
                nsub = nt // 128
                for s_ in range(nsub):
                    for k in range(8):
                        mm(t_[:, hf * T + s_ * 128: hf * T + (s_ + 1) * 128], hT[:, k, s_ * 128:(s_ + 1) * 128], w_[:, k, :],
                           k == 0, k == 7, [wb, B("hT")], [B(name + f"_{hf}")], (k == 7 and s_ == nsub - 1))
                pv = t_[:, hf * T: hf * T + nt].rearrange("p (s c) -> p s c", c=128)
                vd = dsts[key]
                act(vd[:, 0:nsub, 0, 0:64], pv[:, :, 0:64], AF.Copy, [B(name + f"_{hf}")], dsts[key + "_b"])
                act(vd[:, 0:nsub, 1, 64:128], pv[:, :, 64:128], AF.Copy, [B(name + f"_{hf}")], dsts[key + "_b"])

            sl = [0]

            def nxt():
                sl[0] += 1
                return sl[0] % 2
            for i in range(2):
                pb, pn = proj_fm(i)
                act(dsts["a"][:, i, :], pb[:, 0:nt], AF.Copy, [B(pn)], dsts["a_b"])
            for j in range(3):
                pb, pn = proj_fm(2 + j)
                if need_q:
                    qk_post(pb, pn, nt, l, False, 0, latent, dsts["qw"][:, j, :], dsts["qw_b"], nxt())
            pb, pn = proj_fm(5)
            qk_post(pb, pn, nt, l, False, 0, latent, dsts["kw"], dsts["kw_b"], nxt())
            proj_v(6, "vw")
            for j in range(3):
                pb, pn = proj_fm(7 + j)
                if need_q:
                    qk_post(pb, pn, nt, l, True, 0, latent, dsts["qg"][:, j, :], dsts["qg_b"], nxt())
            pb, pn = proj_fm(10)
            qk_post(pb, pn, nt, l, True, 1, latent, dsts["kg"], dsts["kg_b"], nxt())
            proj_v(11, "vg")

        ptc = [0]
        ssc = [0]

        def attend(qsrc, qbufs, nq, tiles, sink_col, ydst, l):
            n = len(tiles)
            sslots = []

            def qk(i):
                kT, kb, _, _, _, pre = tiles[i]
                if pre is not None:
                    pre()
                s_ = ssc[0] % 2
                ssc[0] += 1
                mm(ps_s[s_][:, 0:nq], kT[0:64, :], qsrc[0:64, 0:nq], True, True, kb + qbufs, [B(f"ps_s{s_}_0")], False)
                mm(ps_s[s_][:, T:T + nq], kT[64:128, :], qsrc[64:128, 0:nq], True, True, kb + qbufs, [B(f"ps_s{s_}_1")], True)
                sslots.append(s_)

            qk(0)
            for i in range(n):
                if i + 1 < n:
                    qk(i + 1)
                s_ = sslots[i]
                p_ = ptc[0] % 3
                ptc[0] += 1
                _, _, vv, vb, mk, _ = tiles[i]
                pt = PT[p_]
                ptb = B(f"PT{p_}")
                if nq == T:
                    act(pt[:, :], ps_s[s_][:, :], AF.Exp, [B(f"ps_s{s_}_0"), B(f"ps_s{s_}_1")], [ptb], scale=0.125)
                else:
                    act(pt[:, :].rearrange("p (h t) -> p h t", h=2)[:, :, 0:nq],
                        ps_s[s_][:, :].rearrange("p (h t) -> p h t", h=2)[:, :, 0:nq], AF.Exp,
                        [B(f"ps_s{s_}_0"), B(f"ps_s{s_}_1")], [ptb], scale=0.125)
                if mk is not None:
                    for hh in range(2):
                        tt(pt[:, hh * T:hh * T + nq], pt[:, hh * T:hh * T + nq], mk[:, 0:nq], ALU.mult, [ptb, B("masks")], [ptb])
                mm(ps_acc[0][:, 0:nq], vv[:, 0, :], pt[:, 0:nq], i == 0, i == n - 1, vb + [ptb], [B("ps_acc0_0")], False)
                mm(ps_acc[1][:, 0:nq], vv[:, 1, :], pt[:, T:T + nq], i == 0, i == n - 1, vb + [ptb], [B("ps_acc1_0")], True)
                yield
            vcopy(dsb[0:64, 0:nq], ps_acc[1][0:64, 0:nq], [B("ps_acc1_0")], [B("dsb")])
            vcopy(dsb[64:128, 0:nq], ps_acc[0][64:128, 0:nq], [B("ps_acc0_0")], [B("dsb")])
            w_ = ssc[0] % 2
            ssc[0] += 1
            wps = ps_s[w_][:, 0:nq]
            wpb = B(f"ps_s{w_}_0")
            mm(wps, swap, dsb[:, 0:nq], True, True, [B("consts"), B("dsb")], [wpb], True)
            yield
            if sink_col is not None:
                tsm(rec[:, 0:nq], wps, esink[:, l, sink_col:sink_col + 1], [wpb, B("esink")], [B("rec")], op=ALU.add)
                recip(rec[:, 0:nq], rec[:, 0:nq], [B("rec")], [B("rec")])
            else:
                recip(rec[:, 0:nq], wps, [wpb], [B("rec")])
            tt(ydst[0:64, 0:nq], ps_acc[0][0:64, 0:nq], rec[0:64, 0:nq], ALU.mult, [B("ps_acc0_0"), B("rec")], [B("yT")])
            tt(ydst[64:128, 0:nq], ps_acc[1][64:128, 0:nq], rec[64:128, 0:nq], ALU.mult, [B("ps_acc1_0"), B("rec")], [B("yT")])
            yield

        def pool_mix(nt, l, edges):
            W = nt + 16
            ab = [B("aext")]
            yb = [B("pwt")]
            for c in range(2):
                a_ = aext[:, c, :]
                w2, w4, w8 = pwt[0], pwt[1], pwt[2]
                tt(w2[:, 1:W], a_[:, 1:W], a_[:, 0:W - 1], ALU.add, ab + yb, yb)
                tt(w4[:, 2:W - 1], w2[:, 3:W], w2[:, 1:W - 2], ALU.add, yb, yb)
                if c == 0:
                    srcs = ((0, 64, w2, 2), (64, 128, w4, 4))
                else:
                    tt(w8[:, 4:W - 3], w4[:, 2:W - 5], w4[:, 6:W - 1], ALU.add, yb, yb)
                    tt(w2[64:128, 8:W - 7], w8[64:128, 4:W - 11], w8[64:128, 12:W - 3], ALU.add, yb, yb)
                    srcs = ((0, 64, w8, 8), (64, 128, w2, 16))
                yield
                for (p0, p1, wsrc, wn) in srcs:
                    stt(feat[p0:p1, c, 0:nt], wsrc[p0:p1, 8:8 + nt], 1.0 / wn, a_[p0:p1, 8:8 + nt], ALU.mult, ALU.subtract,
                        yb + ab, [B("feat")])
                    for (idx, side) in edges:
                        c0 = 0 if side == 0 else nt - 8
                        tt(tmp[2][p0:p1, 0:8], wsrc[p0:p1, 8 + c0:16 + c0], icnt[p0:p1, idx, c, side * 8:side * 8 + 8], ALU.mult,
                           yb + [B("icnt")], [B("tmp2")])
                        tt(feat[p0:p1, c, c0:c0 + 8], tmp[2][p0:p1, 0:8], a_[p0:p1, 8 + c0:16 + c0], ALU.subtract,
                           [B("tmp2")] + ab, [B("feat")])
                w_ = ssc[0] % 2
                ssc[0] += 1
                mm(ps_s[w_][:, 0:nt], wpool[:, c, :], feat[:, c, 0:nt], True, True, [B("wpool"), B("feat")], [B(f"ps_s{w_}_0")], True)
                yield
                tsm(yT[:, c, 0:nt], ps_s[w_][:, 0:nt], pscale[:, l, c:c + 1], [B(f"ps_s{w_}_0"), B("pscale")], [B("yT")])
                yield

        def g_stats(src_ap, src_bufs, nt):
            act(sq[:, :, 0:nt], src_ap, AF.Square, src_bufs, [B("actT")])
            yield
            for k in range(8):
                mm(ps_m[0][:, 0:nt], ones_bf, sq[:, k, 0:nt], k == 0, k == 7, [B("cbf"), B("actT")], [B("ps_m0")], k == 7)
            yield
            act(rstd[:, 0:nt], ps_m[0][:, 0:nt], AF.Ln, [B("ps_m0"), B("epsc")], [B("rstd")], bias=epsc[:, 0:1], scale=1.0 / D)
            act(rstd[:, 0:nt], rstd[:, 0:nt], AF.Exp, [B("rstd")], [B("rstd")], scale=-0.5)
            yield

        def g_post_res(nt, l, gi, v, xbuf, xname):
            yield from g_stats(yo[:, :, 0:nt], [B("yo")], nt)
            for k in range(8):
                t = tmp[k % 2]
                tt(t[:, 0:nt], yo[:, k, 0:nt], rstd[:, 0:nt], ALU.mult, [B("yo"), B("rstd")], [B(f"tmp{k % 2}")])
                stt(xbuf[:, k, 0:nt], t[:, 0:nt], tabs[:, l, gi, k, v:v + 1], xbuf[:, k, 0:nt], ALU.mult, ALU.add,
                    [B(f"tmp{k % 2}"), B("tabs"), B(xname)], [B(xname)])
                if k % 2 == 1:
                    yield

        def g_norm_mod(src, src_bufs, nt, l, ti, v):
            yield from g_stats(src[:, :, 0:nt], src_bufs, nt)
            for k in range(8):
                t = tmp[k % 2]
                stt(t[:, 0:nt], src[:, k, 0:nt], tabs[:, l, ti, k, v:v + 1], rstd[:, 0:nt], ALU.mult, ALU.mult,
                    src_bufs + [B("tabs"), B("rstd")], [B(f"tmp{k % 2}")])
                act(hT[:, k, 0:nt], t[:, 0:nt], AF.Identity, [B(f"tmp{k % 2}"), B("tabs")], [B("hT")],
                    bias=tabs[:, l, ti + 1, k, v:v + 1], scale=1.0)
                if k % 2 == 1:
                    yield

        def g_wout(nt, l, ws):
            for m in range(8):
                o = ps_m[m % 2][:, 0:nt]
                w_, wb = ws.get(m)
                for k in range(8):
                    mm(o, w_[:, k, :], yT[:, k, 0:nt], k == 0, k == 7, [wb, B("yT")], [B(f"ps_m{m % 2}")], k == 7)
                act(yo[:, m, 0:nt], o, AF.Copy, [B(f"ps_m{m % 2}")], [B("yo")])
                yield

        def g_tail(nt, l, v, xbuf, xname, ws):
            yield from g_post_res(nt, l, 2, v, xbuf, xname)
            yield from g_norm_mod(xbuf, [B(xname)], nt, l, 3, v)
            dl = [ldB(wdown_s[l, 0], B(f"wdown_s{l}"))]
            og = ps_m[0][:, 0:nt]
            ou = ps_m[1][:, 0:nt]
            for m in range(NFF):
                (wg, wgb), (wu, wub) = ws.get(8 + 2 * m), ws.get(9 + 2 * m)
                s_ = m % 2
                for k in range(8):
                    mm(og, wg[:, k, :], hT[:, k, 0:nt], k == 0, k == 7, [wgb, B("hT")], [B("ps_m0")], k == 7)
                act(sg[s_][:, 0:nt], og, AF.Exp, [B("ps_m0")], [B(f"sg{s_}")], scale=-1.0)
                yield
                for k in range(8):
                    mm(ou, wu[:, k, :], hT[:, k, 0:nt], k == 0, k == 7, [wub, B("hT")], [B("ps_m1")], k == 7)
                tsm(sg[s_][:, 0:nt], sg[s_][:, 0:nt], 1.0, [B(f"sg{s_}")], [B(f"sg{s_}")], op=ALU.add)
                recip(sg[s_][:, 0:nt], sg[s_][:, 0:nt], [B(f"sg{s_}")], [B(f"sg{s_}")])
                tt(sg[s_][:, 0:nt], og, sg[s_][:, 0:nt], ALU.mult, [B("ps_m0"), B(f"sg{s_}")], [B(f"sg{s_}")])
                tt(actT[:, m, 0:nt], sg[s_][:, 0:nt], ou, ALU.mult, [B(f"sg{s_}"), B("ps_m1")], [B("actT")])
                yield
            dl.append(ldB(wdown_s[l, 1], B(f"wdown_s{l}")))
            for mo in range(8):
                wd, wdb = dl[mo]
                a_ = mo % 2
                o = ps_m[a_][:, 0:nt]
                for k in range(NFF):
                    mm(o, wd[:, k, :], actT[:, k, 0:nt], k == 0, k == NFF - 1, [wdb, B("actT")], [B(f"ps_m{a_}")], k == NFF - 1)
                    if k == 10:
                        yield
                if mo + 2 < 8:
                    dl.append(ldB(wdown_s[l, mo + 2], B(f"wdown_s{l}")))
                act(yo[:, mo, 0:nt], o, AF.Copy, [B(f"ps_m{a_}")], [B("yo")])
                yield
            yield from g_post_res(nt, l, 5, v, xbuf, xname)

        def wstream(l):
            return Stream([(wout_s[l, m], B(f"wout_s{l}")) for m in range(8)] +
                          [(w[l, m], B(f"{nm}{l}")) for m in range(NFF) for (w, nm) in ((wgate_s, "wgate_s"), (wup_s, "wup_s"))])

        def drain(g):
            for _ in g:
                pass

        def interleave(ga, gt, ratio):
            a_alive, t_alive = ga is not None, gt is not None
            while a_alive or t_alive:
                if a_alive:
                    for _ in range(ratio):
                        try:
                            next(ga)
                        except StopIteration:
                            a_alive = False
                            break
                if t_alive:
                    try:
                        next(gt)
                    except StopIteration:
                        t_alive = False

        for l in range(L):
            last = (l == FL - 1)
            lp = l % 2
            X_in = xT_in if l == 0 else x_s
            X_out = out_T if l == L - 1 else x_s
            xin_b = (lambda c: B("xT_in")) if l == 0 else (lambda c: B(f"x_s{c}"))
            xout_b = (lambda c: B(f"outT{c}")) if l == L - 1 else (lambda c: B(f"x_s{c}"))

            P.dma("pool", wpool[:], wpool_in[l], reads=[], writes=[B("wpool")])

            norm_mod(xc, [B("xc")], CTX, l, 0, 1)
            in_proj(CTX, l, False, dict(
                a=aext[:, :, 8:8 + CTX], a_b=[B("aext")],
                qw=qwc, qw_b=[B("qwc")], kw=kwc[:, :], kw_b=[B("kwc")],
                qg=qgc, qg_b=[B("qgc")], kg=kgc[:, :], kg_b=[B("kgc")],
                vw=vwc, vw_b=[B("vwc")], vg=vgc, vg_b=[B("vgc")]), need_q=not last)
            if not last:
                P.op("dve", lambda: nc.vector.memset(aext[:, :, 0:8], 0.0), writes=[B("aext")])
                P.op("dve", lambda: nc.vector.memset(aext[:, :, 8 + CTX:16 + CTX], 0.0), writes=[B("aext")])
                drain(pool_mix(CTX, l, [(2, 0), (2, 1)]))
                for j in range(3):
                    tiles = [(kwc[:, i * 128:(i + 1) * 128], [B("kwc")], vwc[:, i], [B("vwc")], None, None) for i in range(2)]
                    drain(attend(qwc[:, j, :], [B("qwc")], CTX, tiles, j, yT[:, 2 + j, :], l))
                for j in range(3):
                    tiles = [(kgc[:, i * 128:(i + 1) * 128], [B("kgc")], vgc[:, i], [B("vgc")], None, None) for i in range(2)]
                    drain(attend(qgc[:, j, :], [B("qgc")], CTX, tiles, None, yT[:, 5 + j, :], l))
                wsx = wstream(l)
                drain(g_wout(CTX, l, wsx))
                drain(g_tail(CTX, l, 1, xc, "xc", wsx))

            def load_x(c):
                load(xa[:], X_in[:, :, c * T:(c + 1) * T].rearrange("k p t -> p k t"), [xin_b(c)], [B("xa")])

            load_x(0)
            for c in range(NCH):
                cs = slice(c * T, (c + 1) * T)
                load(ropeC[:], ropeC_in[:, cs], [], [B("ropeC")])
                load(ropeS[:], ropeS_in[:, cs], [], [B("ropeS")])
                norm_mod(xa, [B("xa")], T, l, 0, 0)
                if c + 1 < NCH:
                    load_x(c + 1)
                in_proj(T, l, True, dict(
                    a=ast, a_b=[B("ast")],
                    qw=qst[0], qw_b=[B("qst0")], kw=kst[0][:, :], kw_b=[B("kst0")],
                    qg=qst[1], qg_b=[B("qst1")], kg=kst[1][:, :], kg_b=[B("kst1")],
                    vw=vst[0], vw_b=[B("vst0")], vg=vst[1], vg_b=[B("vst1")]))
                dp = bass.ds(par_p, 1)
                store(a_t[lp][c][dp].rearrange("o p c t -> p (o c) t"), ast[:], [B("ast")], [B("d_a")])
                store(qw_s[lp, :, :, cs], qst[0][:], [B("qst0")], [B("d_qw")])
                store(qg_s[lp, :, :, cs], qst[1][:], [B("qst1")], [B("d_qg")])
                store(kw_t[lp][c][dp].rearrange("o p t -> p o t"), kst[0][:].rearrange("p (o t) -> p o t", o=1), [B("kst0")], [B("d_kw")])
                store(kg_t[lp][c][dp].rearrange("o p t -> p o t"), kst[1][:].rearrange("p (o t) -> p o t", o=1), [B("kst1")], [B("d_kg")])
                store(vw_t[lp][c][dp].rearrange("o s p f -> p (o s) f"),
                      vst[0][:].rearrange("p s k f -> p s (k f)"), [B("vst0")], [B("d_vw")])
                store(vg_t[lp][c][dp].rearrange("o s p f -> p (o s) f"),
                      vst[1][:].rearrange("p s k f -> p s (k f)"), [B("vst1")], [B("d_vg")])

            P.wait_bufs("pool", [B(n) for n in ("d_a", "d_qw", "d_qg", "d_kw", "d_kg", "d_vw", "d_vg")])
            nc.all_core_barrier()
            if l + 1 < L:
                cast_layer(l + 1)

            groups = [(hh, g) for hh in range(2) for g in range(4)]

            def issue_group(hh, g, s_):
                for u in range(2):
                    load(kgs[s_][:, u * T:(u + 1) * T], kg_t[lp][2 * g + u][hh], [], [B(f"kgs{s_}")])
                for u in range(2):
                    load(vgs[s_][:, 4 * u:4 * u + 4].rearrange("p s k f -> p s (k f)"), vg_t[lp][2 * g + u][hh].rearrange("s p f -> p s f"), [], [B(f"vgs{s_}")])

            def g_attn(c):
                cs = slice(c * T, (c + 1) * T)
                load(qst[0][:], qw_s[lp, :, :, cs], [], [B("qst0")])
                load(qst[1][:], qg_s[lp, :, :, cs], [], [B("qst1")])
                own = bass.ds(par, 1)
                oth = bass.ds(1 - par, 1)
                lo_sel, lo_c = (own, c - 1) if c > 0 else (oth, NCH - 1)
                hi_sel, hi_c = (own, c + 1) if c < NCH - 1 else (oth, 0)

                def k3(ap_):
                    return ap_.rearrange("p (o t) -> p o t", o=1)
                load(k3(kwin[:, 0:128]), kw_t[lp][lo_c][lo_sel, :, T - 128:T].rearrange("o p t -> p o t"), [], [B("kwin")])
                load(k3(kwin[:, 128:640]), kw_t[lp][c][own].rearrange("o p t -> p o t"), [], [B("kwin")])
                load(k3(kwin[:, 640:768]), kw_t[lp][hi_c][hi_sel, :, 0:128].rearrange("o p t -> p o t"), [], [B("kwin")])
                vw3 = vwin[:].rearrange("p s k f -> p s (k f)")
                load(vw3[:, 0:1], vw_t[lp][lo_c][lo_sel, 3:4].rearrange("o s p f -> p (o s) f"), [], [B("vwin")])
                load(vw3[:, 1:5], vw_t[lp][c][own].rearrange("o s p f -> p (o s) f"), [], [B("vwin")])
                load(vw3[:, 5:6], vw_t[lp][hi_c][hi_sel, 0:1].rearrange("o s p f -> p (o s) f"), [], [B("vwin")])
                if c == 0:
                    issue_group(groups[0][0], groups[0][1], 0)
                yield
                for j in range(3):
                    tiles = []
                    for r in range(6):
                        if r == 0:
                            mk = masks[:, 6 if c == 0 else 0, :]
                        elif r == 5:
                            mk = masks[:, 7 if c == NCH - 1 else 5, :]
                        else:
                            mk = masks[:, r, :]
                        tiles.append((kwin[:, r * 128:(r + 1) * 128], [B("kwin")], vwin[:, r], [B("vwin")], mk, None))
                    for i in range(2):
                        tiles.append((kwc[:, i * 128:(i + 1) * 128], [B("kwc")], vwc[:, i], [B("vwc")], None, None))
                    yield from attend(qst[0][:, j, :], [B("qst0")], T, tiles, j, yT[:, 2 + j, :], l)
                for j in range(3):
                    tiles = [(kgc[:, i * 128:(i + 1) * 128], [B("kgc")], vgc[:, i], [B("vgc")], None, None) for i in range(2)]
                    for gi, (hh, g) in enumerate(groups):
                        s_ = gi % 2
                        if gi + 1 < len(groups):
                            nh = groups[gi + 1]
                        elif j + 1 < 3 or c + 1 < NCH:
                            nh = groups[0]
                        else:
                            nh = None
                        for i in range(8):
                            pre = None
                            if i == 2 and nh is not None:
                                pre = (lambda nh=nh, ns=(gi + 1) % 2: issue_group(nh[0], nh[1], ns))
                            tiles.append((kgs[s_][:, i * 128:(i + 1) * 128], [B(f"kgs{s_}")], vgs[s_][:, i], [B(f"vgs{s_}")], None, pre))
                    yield from attend(qst[1][:, j, :], [B("qst1")], T, tiles, None, yT[:, 5 + j, :], l)
                load(aext[:, :, 0:8], a_t[lp][lo_c][lo_sel, :, :, T - 8:T].rearrange("o p c t -> p (o c) t"), [], [B("aext")])
                load(aext[:, :, 8:8 + T], a_t[lp][c][own].rearrange("o p c t -> p (o c) t"), [], [B("aext")])
                load(aext[:, :, 8 + T:16 + T], a_t[lp][hi_c][hi_sel, :, :, 0:8].rearrange("o p c t -> p (o c) t"), [], [B("aext")])
                yield
                edges = []
                if c == 0:
                    tsm(aext[:, :, 0:8], aext[:, :, 0:8], halo[:, 0:1], [B("aext"), B("halo")], [B("aext")])
                    edges.append((0, 0))
                if c == NCH - 1:
                    tsm(aext[:, :, 8 + T:16 + T], aext[:, :, 8 + T:16 + T], halo[:, 1:2], [B("aext"), B("halo")], [B("aext")])
                    edges.append((1, 1))
                yield from pool_mix(T, l, edges)

            drain(g_attn(0))
            for c in range(NCH):
                cs = slice(c * T, (c + 1) * T)
                ws = wstream(l)
                drain(g_wout(T, l, ws))
                load(xa[:], X_in[:, :, cs].rearrange("k p t -> p k t"), [xin_b(c)], [B("xa")])
                interleave(g_attn(c + 1) if c + 1 < NCH else None, g_tail(T, l, 0, xa, "xa", ws), RATIO)
                store(X_out[:, :, cs].rearrange("k p t -> p k t"), xa[:], [B("xa")], [xout_b(c)])

        P.wait_bufs("pool", [B(f"outT{c}") for c in range(NCH)])
    return nc


def _pair_cols(base):
    return [np.r_[base + j * 64: base + j * 64 + 64, base + (3 + j) * 64: base + (3 + j) * 64 + 64] for j in range(3)]


def _cnt(t, w, n):
    lo = np.clip(t - w // 2, 0, n - 1)
    hi = np.clip(t + (w - 1 - w // 2), 0, n - 1)
    return (hi - lo + 1).astype(np.float32)


def _prep_inputs(inp, L):
    f32 = np.float32
    sh = {}
    in_chunks = [np.arange(0, 128), np.arange(128, 256)] + _pair_cols(256) + [np.arange(640, 768), np.arange(768, 896)] \
        + _pair_cols(896) + [np.arange(1280, 1408), np.arange(1408, 1536)]
    cols = np.concatenate(in_chunks)
    w = np.asarray(inp["w_in"], f32)[:L][:, :, cols]
    sh["w_in"] = np.ascontiguousarray(w.reshape(L, 8, 128, 12, 128).transpose(0, 3, 2, 1, 4))
    rows = np.concatenate([np.arange(0, 128), np.arange(128, 256)] + _pair_cols(256) + _pair_cols(640))
    w = np.asarray(inp["w_out"], f32)[:L][:, rows, :]
    sh["w_out"] = np.ascontiguousarray(w.reshape(L, 8, 128, 8, 128).transpose(0, 3, 2, 1, 4))
    for k in ("w_gate", "w_up"):
        w = np.asarray(inp[k], f32)[:L]
        sh[k] = np.ascontiguousarray(w.reshape(L, 8, 128, NFF, 128).transpose(0, 3, 2, 1, 4))
    w = np.asarray(inp["w_down"], f32)[:L]
    sh["w_down"] = np.ascontiguousarray(w.reshape(L, NFF, 128, 8, 128).transpose(0, 3, 2, 1, 4))
    w = np.asarray(inp["w_mod"], f32)[:L]
    sh["wmod"] = np.ascontiguousarray(w.reshape(L, 8, 128, 12, 512).transpose(0, 3, 2, 1, 4))
    b = np.asarray(inp["b_mod"], f32)[:L]
    sh["bmod"] = np.ascontiguousarray(np.stack([b, b], axis=1))
    g = np.stack([np.asarray(inp[k], f32)[:L] for k in ("g_pre_mix", "g_post_mix", "g_pre_ffn", "g_post_ffn")], axis=1)
    sh["gains"] = np.ascontiguousarray(g.reshape(L, 4, 8, 128).transpose(3, 0, 1, 2))
    ps = np.asarray(inp["pool_scale"], f32)[:L]
    sh["pscale"] = np.ascontiguousarray(ps.reshape(L, 2, 128).transpose(2, 0, 1))
    ws = np.asarray(inp["win_sink"], f32)[:L]
    sk = np.zeros((128, L, 3), f32)
    sk[0:64] = ws[None, :, 0:3]
    sk[64:128] = ws[None, :, 3:6]
    sh["sink"] = sk
    gq = np.asarray(inp["g_qnorm"], f32)[:L]
    gk = np.asarray(inp["g_knorm"], f32)[:L]
    gqk = np.zeros((128, L, 2), f32)
    gqk[:, :, 0] = np.concatenate([gq, gq], axis=1).T
    gqk[:, :, 1] = np.concatenate([gk, gk], axis=1).T
    sh["gqk"] = gqk
    wp = np.asarray(inp["w_pool"], f32)[:L]
    wpb = np.zeros((L, 128, 2, 128), f32)
    for c in range(2):
        for gl in range(2):
            wpb[:, gl * 64:(gl + 1) * 64, c, gl * 64:(gl + 1) * 64] = wp[:, 2 * c + gl]
    sh["w_pool"] = wpb
    consts = np.zeros((128, 3, 128), f32)
    for m in range(128):
        d = m % 64
        half = (d % 32) // 16
        partner = m + 16 if half == 0 else m - 16
        consts[partner, 0, m] = 1.0
        consts[(m + 64) % 128, 1, m] = 1.0
    consts[0, 2, 0] = 1.0
    consts[1, 2, 1] = 1.0
    sh["consts"] = consts
    cbf = np.zeros((128, 2, 128), f32)
    cbf[:, 0, :] = 1.0
    cbf[0:64, 1, 0:64] = 1.0
    cbf[64:128, 1, 64:128] = 1.0
    sh["cbf"] = cbf.astype(ml_dtypes.bfloat16)
    jj = np.arange(128)[:, None]
    qq = np.arange(T)[None, :]
    base_masks = np.stack([(np.abs(128 * (r - 1) + jj - qq) <= 128) for r in range(6)], axis=1).astype(f32)
    freq = (np.float32(10000.0) ** (-np.arange(16, dtype=f32) / np.float32(16))).astype(f32)
    x = np.asarray(inp["x"], f32)
    ctx = np.asarray(inp["ctx"], f32)
    cvec = np.asarray(inp["c"], f32)
    cctx = np.asarray(inp["c_ctx"], f32)
    per_core = []
    pwin = (2, 4, 8, 16)
    for r in range(8):
        b_, h_ = r // 2, r % 2
        m = dict(sh)
        xs = x[b_, h_ * HALF:(h_ + 1) * HALF, :]
        m["xT"] = np.ascontiguousarray(xs.T).reshape(8, 128, HALF)
        m["xcT"] = np.ascontiguousarray(ctx[b_].T.reshape(8, 128, CTX).transpose(1, 0, 2))
        cc = np.stack([cvec[b_], cctx], axis=1)
        m["ccT"] = np.ascontiguousarray(cc.reshape(8, 128, 2).transpose(1, 0, 2))
        tg = (h_ * HALF + np.arange(HALF)).astype(f32)
        rowp = np.floor(tg / 64).astype(f32)
        colp = (tg - rowp * 64).astype(f32)
        C = np.zeros((128, HALF), f32)
        S = np.zeros((128, HALF), f32)
        for p in range(128):
            d = p % 64
            axis = d // 32
            half = (d % 32) // 16
            f = d % 16
            ang = ((rowp if axis == 0 else colp) * freq[f]).astype(f32)
            C[p] = np.cos(ang)
            S[p] = np.sin(ang) * (-1.0 if half == 0 else 1.0)
        m["ropeC"] = C
        m["ropeS"] = S
        mk = np.zeros((128, 8, T), f32)
        mk[:, 0:6] = base_masks
        if h_ == 1:
            mk[:, 6] = base_masks[:, 0]
        if h_ == 0:
            mk[:, 7] = base_masks[:, 5]
        m["masks"] = mk.astype(ml_dtypes.bfloat16)
        ic = np.zeros((128, 3, 2, 16), f32)
        for p in range(128):
            for c in range(2):
                w_ = pwin[2 * c + p // 64]
                ic[p, :, c, :] = 1.0 / w_
                t0 = h_ * HALF + np.arange(8)
                ic[p, 0, c, 0:8] = 1.0 / _cnt(t0, w_, SEQ)
                t1 = h_ * HALF + HALF - 8 + np.arange(8)
                ic[p, 1, c, 8:16] = 1.0 / _cnt(t1, w_, SEQ)
                ic[p, 2, c, 0:8] = 1.0 / _cnt(np.arange(8), w_, CTX)
                ic[p, 2, c, 8:16] = 1.0 / _cnt(CTX - 8 + np.arange(8), w_, CTX)
        m["icnt"] = ic
        hl = np.zeros((128, 2), f32)
        hl[:, 0] = 1.0 if h_ == 1 else 0.0
        hl[:, 1] = 1.0 if h_ == 0 else 0.0
        m["halo"] = hl
        per_core.append(m)
    return per_core


_NC_CACHE = {}


def kernel(n_layers=DEPTH, **inputs):
    L = n_layers
    if L not in _NC_CACHE:
        _NC_CACHE[L] = build_program(L)
    nc = _NC_CACHE[L]
    in_maps = _prep_inputs(inputs, L)
    res = run_bass_kernel_spmd(nc, in_maps, core_ids=list(range(8)))
    out = np.empty((BATCH, SEQ, D), np.float32)
    for r in range(8):
        b_, h_ = r // 2, r % 2
        o = np.asarray(res.results[r]["outT"], np.float32).reshape(D, HALF)
        out[b_, h_ * HALF:(h_ + 1) * HALF, :] = o.T
    return out
```

```python
from contextlib import ExitStack
import numpy as np
import ml_dtypes
import concourse.bass as bass
import concourse.mybir as mybir
from concourse.bass_utils import run_bass_kernel_spmd

F32 = mybir.dt.float32
BF16 = mybir.dt.bfloat16
AF = mybir.ActivationFunctionType
ALU = mybir.AluOpType

D = 1024
SEQ = 8192
BATCH = 4
DEPTH = 4
CTX = 256
HALF = SEQ // 2
T = 512
NCH = HALF // T
DFF = 2816
NFF = DFF // 128
EPS = 1e-6
NMOD = 6


class Buf:
    __slots__ = ("name", "wtok", "rtoks", "dsem", "dcount")

    def __init__(self, name):
        self.name = name
        self.wtok = None
        self.rtoks = {}
        self.dsem = None
        self.dcount = 0


class Prog:
    def __init__(self, nc, stack):
        self.nc = nc
        self.stack = stack
        self.engs = {"pe": nc.tensor, "act": nc.scalar, "dve": nc.vector, "pool": nc.gpsimd, "sp": nc.sync}
        self.sems = {}
        self.count = {}
        for e in self.engs:
            self.sems[e] = stack.enter_context(nc.semaphore("s_" + e))
            self.count[e] = 0
        self.waited = {e: {} for e in self.engs}
        self.ninstr = 0

    def _need(self, eng, tok):
        if tok is None:
            return
        key, val = tok
        if eng == "pe" and key == "pe":
            return
        w = self.waited[eng]
        if w.get(key, 0) >= val:
            return
        w[key] = val
        self.engs[eng].wait_ge(self.sems[key], val)

    def _deps(self, eng, reads, writes, skip_key=None):
        for b in reads:
            self._need(eng, b.wtok)
        for b in writes:
            if b.wtok is not None and b.wtok[0] != skip_key:
                self._need(eng, b.wtok)
            for k, v in b.rtoks.items():
                self._need(eng, (k, v))

    @staticmethod
    def _commit(tok, reads, writes):
        k, v = tok
        for b in reads:
            if b.rtoks.get(k, 0) < v:
                b.rtoks[k] = v
        for b in writes:
            b.wtok = tok
            b.rtoks = {}

    def op(self, eng, fn, reads=(), writes=(), inc=True):
        self._deps(eng, reads, writes)
        ins = fn()
        self.ninstr += 1
        if inc:
            self.count[eng] += 1
            ins.then_inc(self.sems[eng], 1)
            tok = (eng, self.count[eng])
        else:
            tok = (eng, self.count[eng] + 1)
        self._commit(tok, reads, writes)
        return tok

    def dma(self, eng, out, in_, reads=(), writes=(), **kw):
        assert len(writes) == 1
        wb = writes[0]
        if wb.dsem is None:
            key = "d_" + wb.name
            self.sems[key] = self.stack.enter_context(self.nc.semaphore(key))
            wb.dsem = key
        self._deps(eng, reads, writes, skip_key=wb.dsem)
        wb.dcount += 16
        ins = self.engs[eng].dma_start(out=out, in_=in_, **kw)
        ins.then_inc(self.sems[wb.dsem], 16)
        self.ninstr += 1
        tok = (wb.dsem, wb.dcount)
        rt = wb.rtoks
        self._commit(tok, reads, writes)
        return tok

    def wait_bufs(self, eng, bufs):
        for b in bufs:
            if b.wtok is not None:
                key, val = b.wtok
                w = self.waited[eng]
                if w.get(key, 0) < val:
                    w[key] = val
                    self.engs[eng].wait_ge(self.sems[key], val)


RATIO = 3


def build_program(n_layers=DEPTH, final_layers=None):
    nc = bass.Bass("TRN2", target_bir_lowering=False, num_devices=8)
    L = n_layers
    FL = L if final_layers is None else final_layers

    def din(name, shape, dt=F32):
        return nc.dram_tensor(name, list(shape), dt, kind="ExternalInput").ap()

    xT_in = din("xT", [8, 128, HALF])
    xcT_in = din("xcT", [128, 8, CTX])
    ccT_in = din("ccT", [128, 8, 2])
    wmod_in = din("wmod", [L, 12, 128, 8, 512])
    bmod_in = din("bmod", [L, 2, 6144])
    gains_in = din("gains", [128, L, 4, 8])
    pscale_in = din("pscale", [128, L, 2])
    sink_in = din("sink", [128, L, 3])
    gqk_in = din("gqk", [128, L, 2])
    win_in = din("w_in", [L, 12, 128, 8, 128])
    wout_in = din("w_out", [L, 8, 128, 8, 128])
    wgate_in = din("w_gate", [L, NFF, 128, 8, 128])
    wup_in = din("w_up", [L, NFF, 128, 8, 128])
    wdown_in = din("w_down", [L, 8, 128, NFF, 128])
    wpool_in = din("w_pool", [L, 128, 2, 128])
    ropeC_in = din("ropeC", [128, HALF])
    ropeS_in = din("ropeS", [128, HALF])
    consts_in = din("consts", [128, 3, 128])
    cbf_in = din("cbf", [128, 3, 128], BF16)
    masks_in = din("masks", [128, 8, T], BF16)
    icnt_in = din("icnt", [128, 3, 2, 16])
    halo_in = din("halo", [128, 2])
    out_T = nc.dram_tensor("outT", [8, 128, HALF], F32, kind="ExternalOutput").ap()

    def dint(name, shape, dt, shared=False):
        if shared:
            return nc.dram_tensor(name, list(shape), dt, addr_space="Shared").ap()
        return nc.dram_tensor(name, list(shape), dt).ap()

    x_s = dint("x_s", [8, 128, HALF], F32)
    win_s = dint("win_s", [L, 12, 128, 8, 128], BF16)
    wout_s = dint("wout_s", [L, 8, 128, 8, 128], BF16)
    wgate_s = dint("wgate_s", [L, NFF, 128, 8, 128], BF16)
    wup_s = dint("wup_s", [L, NFF, 128, 8, 128], BF16)
    wdown_s = dint("wdown_s", [L, 8, 128, NFF, 128], BF16)
    qw_s = dint("qw_s", [2, 128, 3, HALF], BF16)
    qg_s = dint("qg_s", [2, 128, 3, HALF], BF16)
    kw_t = [[dint(f"kw_sh{p}_{c}", [2, 128, T], BF16, True) for c in range(NCH)] for p in range(2)]
    kg_t = [[dint(f"kg_sh{p}_{c}", [2, 128, T], BF16, True) for c in range(NCH)] for p in range(2)]
    vw_t = [[dint(f"vw_sh{p}_{c}", [2, 4, 128, 256], BF16, True) for c in range(NCH)] for p in range(2)]
    vg_t = [[dint(f"vg_sh{p}_{c}", [2, 4, 128, 256], BF16, True) for c in range(NCH)] for p in range(2)]
    a_t = [[dint(f"a_sh{p}_{c}", [2, 128, 2, T], F32, True) for c in range(NCH)] for p in range(2)]

    st = ExitStack()
    with st:
        P = Prog(nc, st)
        bufs = {}

        def B(name):
            if name not in bufs:
                bufs[name] = Buf(name)
            return bufs[name]

        def sb(name, shape, dt):
            return st.enter_context(nc.sbuf_tensor("sb_" + name, list(shape), dt))

        def psum(name, shape):
            return st.enter_context(nc.psum_tensor(name, list(shape), F32))

        par = nc.sync.partition_id() % 2
        par_p = nc.gpsimd.partition_id() % 2

        consts = sb("consts", [128, 3, 128], F32)
        cbf = sb("cbf", [128, 3, 128], BF16)
        masks = sb("masks", [128, 8, T], BF16)
        icnt = sb("icnt", [128, 3, 2, 16], F32)
        halo = sb("halo", [128, 2], F32)
        gains = sb("gains", [128, L, 4, 8], F32)
        pscale = sb("pscale", [128, L, 2], F32)
        esink = sb("esink", [128, L, 3], F32)
        gqk = sb("gqk", [128, L, 2], F32)
        epsc = sb("epsc", [128, 1], F32)
        modT = sb("modT", [128, L, 48, 2], F32)
        tabs = sb("tabs", [128, L, 6, 8, 2], F32)
        ccT = sb("ccT", [128, 8, 2], F32)
        scc = sb("scc", [128, 8, 2], F32)
        sccb = sb("sccb", [128, 8, 2], BF16)
        xc = sb("xc", [128, 8, CTX], F32)
        wpool = sb("wpool", [128, 2, 128], BF16)
        kwc = sb("kwc", [128, CTX], BF16)
        kgc = sb("kgc", [128, CTX], BF16)
        vwc = sb("vwc", [128, 2, 2, 128], BF16)
        vgc = sb("vgc", [128, 2, 2, 128], BF16)
        qwc = sb("qwc", [128, 3, CTX], BF16)
        qgc = sb("qgc", [128, 3, CTX], BF16)
        aext = sb("aext", [128, 2, T + 16], F32)
        xa = sb("xa", [128, 8, T], F32)
        hT = sb("hT", [128, 8, T], BF16)
        rstd = sb("rstd", [128, T], F32)
        tmp = [sb(f"tmp{i}", [128, T], F32) for i in range(4)]
        qf = [sb(f"qf{i}", [128, T], F32) for i in range(2)]
        sqh = sb("sqh", [128, T], BF16)
        ropeC = sb("ropeC", [128, T], F32)
        ropeS = sb("ropeS", [128, T], F32)
        ringA = [sb(f"ringA{i}", [128, 8, 128], BF16) for i in range(8)]
        ringB = [sb(f"ringB{i}", [128, NFF, 128], BF16) for i in range(2)]
        qst = [sb(f"qst{i}", [128, 3, T], BF16) for i in range(2)]
        kst = [sb(f"kst{i}", [128, T], BF16) for i in range(2)]
        vst = [sb(f"vst{i}", [128, 4, 2, 128], BF16) for i in range(2)]
        ast = sb("ast", [128, 2, T], F32)
        kwin = sb("kwin", [128, 6 * 128], BF16)
        vwin = sb("vwin", [128, 6, 2, 128], BF16)
        kgs = [sb(f"kgs{i}", [128, 1024], BF16) for i in range(2)]
        vgs = [sb(f"vgs{i}", [128, 8, 2, 128], BF16) for i in range(2)]
        PT = [sb(f"PT{i}", [128, 2 * T], BF16) for i in range(3)]
        dsb = sb("dsb", [128, T], F32)
        rec = sb("rec", [128, T], F32)
        yT = sb("yT", [128, 8, T], BF16)
        yo = sb("yo", [128, 8, T], F32)
        actT = sb("actT", [128, NFF, T], BF16)
        sg = [sb(f"sg{i}", [128, T], F32) for i in range(2)]
        feat = sb("feat", [128, 2, T], BF16)
        sq = actT[:, 0:8, :]
        yo_flat = yo[:].rearrange("p k t -> p (k t)")
        pwt = [sb(f"pwt{i}", [128, T + 16], F32) for i in range(3)]

        ps_s = [psum(f"ps_s{i}", [128, 2 * T]) for i in range(2)]
        ps_acc = [psum(f"ps_acc{i}", [128, T]) for i in range(2)]
        ps_m = [psum(f"ps_m{i}", [128, T]) for i in range(2)]

        def mm(out, lhsT, rhs, start, stop, reads, writes, inc):
            return P.op("pe", lambda: nc.tensor.matmul(out, lhsT, rhs, start=start, stop=stop),
                        reads=reads, writes=writes, inc=inc)

        def act(out, in_, func, reads, writes, bias=None, scale=None):
            kw = {}
            if bias is not None:
                kw["bias"] = bias
            if scale is not None:
                kw["scale"] = scale
            return P.op("act", lambda: nc.scalar.activation(out=out, in_=in_, func=func, **kw), reads=reads, writes=writes)

        def tt(out, in0, in1, op, reads, writes):
            return P.op("dve", lambda: nc.vector.tensor_tensor(out, in0, in1, op), reads=reads, writes=writes)

        def stt(out, in0, scalar, in1, op0, op1, reads, writes):
            return P.op("dve", lambda: nc.vector.scalar_tensor_tensor(out, in0, scalar, in1, op0, op1), reads=reads, writes=writes)

        def tsm(out, in0, s1, reads, writes, op=ALU.mult):
            return P.op("dve", lambda: nc.vector.tensor_scalar(out, in0, s1, None, op), reads=reads, writes=writes)

        def recip(out, in_, reads, writes):
            return P.op("dve", lambda: nc.vector.reciprocal(out, in_), reads=reads, writes=writes)

        def vcopy(out, in_, reads, writes):
            return P.op("dve", lambda: nc.vector.tensor_copy(out, in_), reads=reads, writes=writes)

        def load(out, in_, reads, writes):
            return P.dma("sp", out, in_, reads=reads, writes=writes)

        def store(out, in_, reads, writes):
            return P.dma("pool", out, in_, reads=reads, writes=writes)

        load(consts[:], consts_in, [], [B("consts")])
        load(cbf[:], cbf_in, [], [B("cbf")])
        load(masks[:], masks_in, [], [B("masks")])
        load(icnt[:], icnt_in, [], [B("icnt")])
        load(halo[:], halo_in, [], [B("halo")])
        load(gains[:], gains_in, [], [B("gains")])
        load(pscale[:], pscale_in, [], [B("pscale")])
        load(esink[:], sink_in, [], [B("esink")])
        load(gqk[:], gqk_in, [], [B("gqk")])
        load(ccT[:], ccT_in, [], [B("ccT")])
        load(xc[:], xcT_in, [], [B("xc")])
        P.op("dve", lambda: nc.vector.memset(epsc[:], EPS), writes=[B("epsc")])
        for i in range(2):
            P.op("dve", lambda i=i: nc.vector.memset(vst[i][:], 1.0), writes=[B(f"vst{i}")])
        P.op("dve", lambda: nc.vector.memset(vwc[:], 1.0), writes=[B("vwc")])
        P.op("dve", lambda: nc.vector.memset(vgc[:], 1.0), writes=[B("vgc")])
        act(esink[:], esink[:], AF.Exp, [B("esink")], [B("esink")])
        act(scc[:], ccT[:], AF.Exp, [B("ccT")], [B("scc")], scale=-1.0)
        tsm(scc[:], scc[:], 1.0, [B("scc")], [B("scc")], op=ALU.add)
        recip(scc[:], scc[:], [B("scc")], [B("scc")])
        tt(scc[:], ccT[:], scc[:], ALU.mult, [B("ccT"), B("scc")], [B("scc")])
        vcopy(sccb[:], scc[:], [B("scc")], [B("sccb")])
        perm = consts[:, 0, :]
        swap = consts[:, 1, :]
        eye2 = consts[0:2, 2, 0:2]
        ones_bf = cbf[:, 0, :]
        bd_bf = cbf[:, 1, :]
        ident_bf = cbf[:, 2, :]

        def cast_layer(l):
            def cast(dst, src, name, nsplit):
                n0 = dst.shape[0]
                step = max(1, n0 // nsplit)
                for i in range(0, n0, step):
                    P.dma("pool", dst[i:i + step], src[i:i + step], reads=[], writes=[B(name)])
            cast(win_s[l], win_in[l], f"win_s{l}", 4)
            cast(wout_s[l], wout_in[l], f"wout_s{l}", 2)
            cast(wgate_s[l], wgate_in[l], f"wgate_s{l}", 11)
            cast(wup_s[l], wup_in[l], f"wup_s{l}", 11)
            cast(wdown_s[l], wdown_in[l], f"wdown_s{l}", 8)

        cast_layer(0)

        mrows = [tmp[0], tmp[1], tmp[2], tmp[3]]
        brows = [qf[0], qf[1], sg[0], sg[1]]
        mrn = ["tmp0", "tmp1", "tmp2", "tmp3"]
        brn = ["qf0", "qf1", "sg0", "sg1"]
        mst = [(xa, "xa"), (yo, "yo")]
        msi = 0
        for l in range(L):
            for g3 in range(3):
                for n4 in range(4):
                    n = g3 * 4 + n4
                    load(brows[n4][0:2, :], bmod_in[l, :, n * 512:(n + 1) * 512], [], [B(brn[n4])])
                    ms_, msn = mst[msi % 2]
                    msi += 1
                    load(ms_[:], wmod_in[l, n], [], [B(msn)])
                    for k in range(8):
                        mm(ps_m[0][0:2, :], scc[:, k, :], ms_[:, k, :], k == 0, k == 7,
                           [B("scc"), B(msn)], [B("ps_m0")], k == 7)
                    tt(mrows[n4][0:2, :], ps_m[0][0:2, :], brows[n4][0:2, :], ALU.add,
                       [B("ps_m0"), B(brn[n4])], [B(mrn[n4])])
                for jj in range(16):
                    j = g3 * 16 + jj
                    mm(ps_m[1][:, 2 * j:2 * j + 2], mrows[jj // 4][0:2, (jj % 4) * 128:(jj % 4 + 1) * 128], eye2, True, True,
                       [B(mrn[jj // 4]), B("consts")], [B("ps_m1")], True)
            vcopy(modT[:, l, :, :], ps_m[1][:, 0:96].rearrange("p (j v) -> p j v", v=2), [B("ps_m1")], [B("modT")])
            for v in range(2):
                for s, (gi_pre, gi_post) in enumerate(((0, 1), (2, 3))):
                    m0 = 3 * s
                    stt(tabs[:, l, 3 * s + 0, :, v], modT[:, l, (m0 + 1) * 8:(m0 + 2) * 8, v], 1.0, gains[:, l, gi_pre, :],
                        ALU.add, ALU.mult, [B("modT"), B("gains")], [B("tabs")])
                    vcopy(tabs[:, l, 3 * s + 1, :, v], modT[:, l, m0 * 8:(m0 + 1) * 8, v], [B("modT")], [B("tabs")])
                    tt(tabs[:, l, 3 * s + 2, :, v], modT[:, l, (m0 + 2) * 8:(m0 + 3) * 8, v], gains[:, l, gi_post, :], ALU.mult,
                       [B("modT"), B("gains")], [B("tabs")])

        def rms_stats(src_ap, src_bufs, nt, width):
            act(sq[:, :, 0:nt], src_ap, AF.Square, src_bufs, [B("actT")])
            for k in range(8):
                mm(ps_m[0][:, 0:nt], ones_bf, sq[:, k, 0:nt], k == 0, k == 7, [B("cbf"), B("actT")], [B("ps_m0")], k == 7)
            act(rstd[:, 0:nt], ps_m[0][:, 0:nt], AF.Ln, [B("ps_m0"), B("epsc")], [B("rstd")], bias=epsc[:, 0:1], scale=1.0 / width)
            act(rstd[:, 0:nt], rstd[:, 0:nt], AF.Exp, [B("rstd")], [B("rstd")], scale=-0.5)

        def norm_mod(src, src_bufs, nt, l, ti, v):
            rms_stats(src[:, :, 0:nt], src_bufs, nt, D)
            for k in range(8):
                t = tmp[k % 2]
                stt(t[:, 0:nt], src[:, k, 0:nt], tabs[:, l, ti, k, v:v + 1], rstd[:, 0:nt], ALU.mult, ALU.mult,
                    src_bufs + [B("tabs"), B("rstd")], [B(f"tmp{k % 2}")])
                act(hT[:, k, 0:nt], t[:, 0:nt], AF.Identity, [B(f"tmp{k % 2}"), B("tabs")], [B("hT")],
                    bias=tabs[:, l, ti + 1, k, v:v + 1], scale=1.0)

        def post_res(nt, l, gi, v, xbuf, xname):
            rms_stats(yo[:, :, 0:nt], [B("yo")], nt, D)
            for k in range(8):
                t = tmp[k % 2]
                tt(t[:, 0:nt], yo[:, k, 0:nt], rstd[:, 0:nt], ALU.mult, [B("yo"), B("rstd")], [B(f"tmp{k % 2}")])
                stt(xbuf[:, k, 0:nt], t[:, 0:nt], tabs[:, l, gi, k, v:v + 1], xbuf[:, k, 0:nt], ALU.mult, ALU.add,
                    [B(f"tmp{k % 2}"), B("tabs"), B(xname)], [B(xname)])

        ra = [0]
        rb = [0]

        def ldA(src_ap, srcb):
            i = ra[0] % 8
            ra[0] += 1
            load(ringA[i][:], src_ap, [srcb], [B(f"ringA{i}")])
            return ringA[i], B(f"ringA{i}")

        def ldB(src_ap, srcb):
            i = rb[0] % 2
            rb[0] += 1
            load(ringB[i][:], src_ap, [srcb], [B(f"ringB{i}")])
            return ringB[i], B(f"ringB{i}")

        class Stream:
            def __init__(self, seq, ahead=7):
                self.seq = seq
                self.loaded = []
                self.ahead = ahead

            def get(self, i):
                while len(self.loaded) < min(len(self.seq), i + self.ahead):
                    self.loaded.append(ldA(*self.seq[len(self.loaded)]))
                return self.loaded[i]

        def qk_post(psb, psname, nt, l, is_glb, gcol, rope, dst, dst_bufs, slot):
            if slot == 0:
                f, fb = qf[0], B("qf0")
                sq_, sqb = sqh, B("sqh")
                rs_, rsb = rstd, B("rstd")
                pa, pab = ps_m[0], B("ps_m0")
                pb_, pbb = ps_m[1], B("ps_m1")
                t2, t2b = tmp[2], B("tmp2")
                t3, t3b = tmp[3], B("tmp3")
            else:
                f, fb = qf[1], B("qf1")
                sq_, sqb = PT[0], B("PT0")
                rs_, rsb = rec, B("rec")
                pa, pab = ps_acc[0], B("ps_acc0_0")
                pb_, pbb = ps_acc[1], B("ps_acc1_0")
                t2, t2b = dsb, B("dsb")
                t3, t3b = sg[0], B("sg0")
            if is_glb:
                act(sq_[:, 0:nt], psb[:, 0:nt], AF.Square, [B(psname)], [sqb])
                mm(pa[:, 0:nt], bd_bf, sq_[:, 0:nt], True, True, [B("cbf"), sqb], [pab], True)
                act(rs_[:, 0:nt], pa[:, 0:nt], AF.Ln, [pab, B("epsc")], [rsb], bias=epsc[:, 0:1], scale=1.0 / 64)
                act(rs_[:, 0:nt], rs_[:, 0:nt], AF.Exp, [rsb], [rsb], scale=-0.5)
                stt(f[:, 0:nt] if rope else dst, psb[:, 0:nt], gqk[:, l, gcol:gcol + 1], rs_[:, 0:nt], ALU.mult, ALU.mult,
                    [B(psname), B("gqk"), rsb], [fb] if rope else dst_bufs)
            else:
                act(f[:, 0:nt] if rope else dst, psb[:, 0:nt], AF.Copy, [B(psname)], [fb] if rope else dst_bufs)
            if rope:
                mm(pb_[:, 0:nt], perm, f[:, 0:nt], True, True, [B("consts"), fb], [pbb], True)
                tt(t2[:, 0:nt], f[:, 0:nt], ropeC[:, 0:nt], ALU.mult, [fb, B("ropeC")], [t2b])
                tt(t3[:, 0:nt], pb_[:, 0:nt], ropeS[:, 0:nt], ALU.mult, [pbb, B("ropeS")], [t3b])
                tt(dst, t2[:, 0:nt], t3[:, 0:nt], ALU.add, [t2b, t3b], dst_bufs)

        PSROT = [(ps_s[0], "ps_s0", 0), (ps_s[0], "ps_s0", 1), (ps_s[1], "ps_s1", 0), (ps_s[1], "ps_s1", 1)]

        def in_proj(nt, l, latent, dsts, need_q=True):
            ws = Stream([(win_s[l, mc], B(f"win_s{l}")) for mc in range(12)])
            rot = [0]

            def proj_fm(mc):
                t_, name, hf = PSROT[rot[0] % len(PSROT)]
                rot[0] += 1
                o = t_[:, hf * T:hf * T + nt]
                w_, wb = ws.get(mc)
                for k in range(8):
                    mm(o, w_[:, k, :], hT[:, k, 0:nt], k == 0, k == 7, [wb, B("hT")], [B(name + f"_{hf}")], k == 7)
                return t_[:, hf * T:(hf + 1) * T], name + f"_{hf}"

            def proj_v(mc, key):
                t_, name, hf = PSROT[rot[0] % len(PSROT)]
                rot[0] += 1
                w_, wb = ws.get(mc)
                nsub = nt // 128
                for s_ in range(nsub):
                    for k in range(8):
                        mm(t_[:, hf * T + s_ * 128: hf * T + (s_ + 1) * 128], hT[:, k, s_ * 128:(s_ + 1) * 128], w_[:, k, :],
                           k == 0, k == 7, [wb, B("hT")], [B(name + f"_{hf}")], (k == 7 and s_ == nsub - 1))
                pv = t_[:, hf * T: hf * T + nt].rearrange("p (s c) -> p s c", c=128)
                vd = dsts[key]
                act(vd[:, 0:nsub, 0, 0:64], pv[:, :, 0:64], AF.Copy, [B(name + f"_{hf}")], dsts[key + "_b"])
                act(vd[:, 0:nsub, 1, 64:128], pv[:, :, 64:128], AF.Copy, [B(name + f"_{hf}")], dsts[key + "_b"])

            sl = [0]

            def nxt():
                sl[0] += 1
                return sl[0] % 2
            for i in range(2):
                pb, pn = proj_fm(i)
                act(dsts["a"][:, i, :], pb[:, 0:nt], AF.Copy, [B(pn)], dsts["a_b"])
            for j in range(3):
                pb, pn = proj_fm(2 + j)
                if need_q:
                    qk_post(pb, pn, nt, l, False, 0, latent, dsts["qw"][:, j, :], dsts["qw_b"], nxt())
            pb, pn = proj_fm(5)
            qk_post(pb, pn, nt, l, False, 0, latent, dsts["kw"], dsts["kw_b"], nxt())
            proj_v(6, "vw")
            for j in range(3):
                pb, pn = proj_fm(7 + j)
                if need_q:
                    qk_post(pb, pn, nt, l, True, 0, latent, dsts["qg"][:, j, :], dsts["qg_b"], nxt())
            pb, pn = proj_fm(10)
            qk_post(pb, pn, nt, l, True, 1, latent, dsts["kg"], dsts["kg_b"], nxt())
            proj_v(11, "vg")

        ptc = [0]
        ssc = [0]

        def attend(qsrc, qbufs, nq, tiles, sink_col, ydst, l):
            n = len(tiles)
            sslots = []

            def qk(i):
                kT, kb, _, _, _, pre = tiles[i]
                if pre is not None:
                    pre()
                s_ = ssc[0] % 2
                ssc[0] += 1
                mk_ = tiles[i][4]
                mm(ps_s[s_][:, 0:nq], kT[0:64, :], qsrc[0:64, 0:nq], True, mk_ is None, kb + qbufs, [B(f"ps_s{s_}_0")], False)
                mm(ps_s[s_][:, T:T + nq], kT[64:128, :], qsrc[64:128, 0:nq], True, mk_ is None, kb + qbufs, [B(f"ps_s{s_}_1")], mk_ is None)
                if mk_ is not None:
                    mm(ps_s[s_][:, 0:nq], ident_bf, mk_[:, 0:nq], False, True, [B("cbf"), B("masks")], [B(f"ps_s{s_}_0")], False)
                    mm(ps_s[s_][:, T:T + nq], ident_bf, mk_[:, 0:nq], False, True, [B("cbf"), B("masks")], [B(f"ps_s{s_}_1")], True)
                sslots.append(s_)

            qk(0)
            for i in range(n):
                if i + 1 < n:
                    qk(i + 1)
                s_ = sslots[i]
                p_ = ptc[0] % 3
                ptc[0] += 1
                _, _, vv, vb, mk, _ = tiles[i]
                pt = PT[p_]
                ptb = B(f"PT{p_}")
                if nq == T:
                    act(pt[:, :], ps_s[s_][:, :], AF.Exp, [B(f"ps_s{s_}_0"), B(f"ps_s{s_}_1")], [ptb], scale=0.125)
                else:
                    act(pt[:, :].rearrange("p (h t) -> p h t", h=2)[:, :, 0:nq],
                        ps_s[s_][:, :].rearrange("p (h t) -> p h t", h=2)[:, :, 0:nq], AF.Exp,
                        [B(f"ps_s{s_}_0"), B(f"ps_s{s_}_1")], [ptb], scale=0.125)
                mm(ps_acc[0][:, 0:nq], vv[:, 0, :], pt[:, 0:nq], i == 0, i == n - 1, vb + [ptb], [B("ps_acc0_0")], False)
                mm(ps_acc[1][:, 0:nq], vv[:, 1, :], pt[:, T:T + nq], i == 0, i == n - 1, vb + [ptb], [B("ps_acc1_0")], True)
                yield
            vcopy(dsb[0:64, 0:nq], ps_acc[1][0:64, 0:nq], [B("ps_acc1_0")], [B("dsb")])
            vcopy(dsb[64:128, 0:nq], ps_acc[0][64:128, 0:nq], [B("ps_acc0_0")], [B("dsb")])
            w_ = ssc[0] % 2
            ssc[0] += 1
            wps = ps_s[w_][:, 0:nq]
            wpb = B(f"ps_s{w_}_0")
            mm(wps, swap, dsb[:, 0:nq], True, True, [B("consts"), B("dsb")], [wpb], True)
            yield
            if sink_col is not None:
                tsm(rec[:, 0:nq], wps, esink[:, l, sink_col:sink_col + 1], [wpb, B("esink")], [B("rec")], op=ALU.add)
                recip(rec[:, 0:nq], rec[:, 0:nq], [B("rec")], [B("rec")])
            else:
                recip(rec[:, 0:nq], wps, [wpb], [B("rec")])
            tt(ydst[0:64, 0:nq], ps_acc[0][0:64, 0:nq], rec[0:64, 0:nq], ALU.mult, [B("ps_acc0_0"), B("rec")], [B("yT")])
            tt(ydst[64:128, 0:nq], ps_acc[1][64:128, 0:nq], rec[64:128, 0:nq], ALU.mult, [B("ps_acc1_0"), B("rec")], [B("yT")])
            yield

        def pool_mix(nt, l, edges):
            W = nt + 16
            ab = [B("aext")]
            yb = [B("pwt")]
            for c in range(2):
                a_ = aext[:, c, :]
                w2, w4, w8 = pwt[0], pwt[1], pwt[2]
                tt(w2[:, 1:W], a_[:, 1:W], a_[:, 0:W - 1], ALU.add, ab + yb, yb)
                tt(w4[:, 2:W - 1], w2[:, 3:W], w2[:, 1:W - 2], ALU.add, yb, yb)
                if c == 0:
                    srcs = ((0, 64, w2, 2), (64, 128, w4, 4))
                else:
                    tt(w8[:, 4:W - 3], w4[:, 2:W - 5], w4[:, 6:W - 1], ALU.add, yb, yb)
                    tt(w2[64:128, 8:W - 7], w8[64:128, 4:W - 11], w8[64:128, 12:W - 3], ALU.add, yb, yb)
                    srcs = ((0, 64, w8, 8), (64, 128, w2, 16))
                yield
                for (p0, p1, wsrc, wn) in srcs:
                    stt(feat[p0:p1, c, 0:nt], wsrc[p0:p1, 8:8 + nt], 1.0 / wn, a_[p0:p1, 8:8 + nt], ALU.mult, ALU.subtract,
                        yb + ab, [B("feat")])
                    for (idx, side) in edges:
                        c0 = 0 if side == 0 else nt - 8
                        tt(tmp[2][p0:p1, 0:8], wsrc[p0:p1, 8 + c0:16 + c0], icnt[p0:p1, idx, c, side * 8:side * 8 + 8], ALU.mult,
                           yb + [B("icnt")], [B("tmp2")])
                        tt(feat[p0:p1, c, c0:c0 + 8], tmp[2][p0:p1, 0:8], a_[p0:p1, 8 + c0:16 + c0], ALU.subtract,
                           [B("tmp2")] + ab, [B("feat")])
                w_ = ssc[0] % 2
                ssc[0] += 1
                mm(ps_s[w_][:, 0:nt], wpool[:, c, :], feat[:, c, 0:nt], True, True, [B("wpool"), B("feat")], [B(f"ps_s{w_}_0")], True)
                yield
                tsm(yT[:, c, 0:nt], ps_s[w_][:, 0:nt], pscale[:, l, c:c + 1], [B(f"ps_s{w_}_0"), B("pscale")], [B("yT")])
                yield

        def g_stats(src_ap, src_bufs, nt):
            act(sq[:, :, 0:nt], src_ap, AF.Square, src_bufs, [B("actT")])
            yield
            for k in range(8):
                mm(ps_m[0][:, 0:nt], ones_bf, sq[:, k, 0:nt], k == 0, k == 7, [B("cbf"), B("actT")], [B("ps_m0")], k == 7)
            yield
            act(rstd[:, 0:nt], ps_m[0][:, 0:nt], AF.Ln, [B("ps_m0"), B("epsc")], [B("rstd")], bias=epsc[:, 0:1], scale=1.0 / D)
            act(rstd[:, 0:nt], rstd[:, 0:nt], AF.Exp, [B("rstd")], [B("rstd")], scale=-0.5)
            yield

        def g_post_res(nt, l, gi, v, xbuf, xname):
            yield from g_stats(yo[:, :, 0:nt], [B("yo")], nt)
            for k in range(8):
                t = tmp[k % 2]
                tt(t[:, 0:nt], yo[:, k, 0:nt], rstd[:, 0:nt], ALU.mult, [B("yo"), B("rstd")], [B(f"tmp{k % 2}")])
                stt(xbuf[:, k, 0:nt], t[:, 0:nt], tabs[:, l, gi, k, v:v + 1], xbuf[:, k, 0:nt], ALU.mult, ALU.add,
                    [B(f"tmp{k % 2}"), B("tabs"), B(xname)], [B(xname)])
                if k % 2 == 1:
                    yield

        def g_norm_mod(src, src_bufs, nt, l, ti, v):
            yield from g_stats(src[:, :, 0:nt], src_bufs, nt)
            for k in range(8):
                t = tmp[k % 2]
                stt(t[:, 0:nt], src[:, k, 0:nt], tabs[:, l, ti, k, v:v + 1], rstd[:, 0:nt], ALU.mult, ALU.mult,
                    src_bufs + [B("tabs"), B("rstd")], [B(f"tmp{k % 2}")])
                act(hT[:, k, 0:nt], t[:, 0:nt], AF.Identity, [B(f"tmp{k % 2}"), B("tabs")], [B("hT")],
                    bias=tabs[:, l, ti + 1, k, v:v + 1], scale=1.0)
                if k % 2 == 1:
                    yield

        def g_wout(nt, l, ws):
            for m in range(8):
                o = ps_m[m % 2][:, 0:nt]
                w_, wb = ws.get(m)
                for k in range(8):
                    mm(o, w_[:, k, :], yT[:, k, 0:nt], k == 0, k == 7, [wb, B("yT")], [B(f"ps_m{m % 2}")], k == 7)
                act(yo[:, m, 0:nt], o, AF.Copy, [B(f"ps_m{m % 2}")], [B("yo")])
                yield

        def g_tail(nt, l, v, xbuf, xname, ws):
            yield from g_post_res(nt, l, 2, v, xbuf, xname)
            yield from g_norm_mod(xbuf, [B(xname)], nt, l, 3, v)
            dl = [ldB(wdown_s[l, 0], B(f"wdown_s{l}"))]
            og = ps_m[0][:, 0:nt]
            ou = ps_m[1][:, 0:nt]
            for m in range(NFF):
                (wg, wgb), (wu, wub) = ws.get(8 + 2 * m), ws.get(9 + 2 * m)
                s_ = m % 2
                for k in range(8):
                    mm(og, wg[:, k, :], hT[:, k, 0:nt], k == 0, k == 7, [wgb, B("hT")], [B("ps_m0")], k == 7)
                act(sg[s_][:, 0:nt], og, AF.Exp, [B("ps_m0")], [B(f"sg{s_}")], scale=-1.0)
                yield
                for k in range(8):
                    mm(ou, wu[:, k, :], hT[:, k, 0:nt], k == 0, k == 7, [wub, B("hT")], [B("ps_m1")], k == 7)
                tsm(sg[s_][:, 0:nt], sg[s_][:, 0:nt], 1.0, [B(f"sg{s_}")], [B(f"sg{s_}")], op=ALU.add)
                recip(sg[s_][:, 0:nt], sg[s_][:, 0:nt], [B(f"sg{s_}")], [B(f"sg{s_}")])
                tt(sg[s_][:, 0:nt], og, sg[s_][:, 0:nt], ALU.mult, [B("ps_m0"), B(f"sg{s_}")], [B(f"sg{s_}")])
                tt(actT[:, m, 0:nt], sg[s_][:, 0:nt], ou, ALU.mult, [B(f"sg{s_}"), B("ps_m1")], [B("actT")])
                yield
            dl.append(ldB(wdown_s[l, 1], B(f"wdown_s{l}")))
            for mo in range(8):
                wd, wdb = dl[mo]
                a_ = mo % 2
                o = ps_m[a_][:, 0:nt]
                for k in range(NFF):
                    mm(o, wd[:, k, :], actT[:, k, 0:nt], k == 0, k == NFF - 1, [wdb, B("actT")], [B(f"ps_m{a_}")], k == NFF - 1)
                    if k == 10:
                        yield
                if mo + 2 < 8:
                    dl.append(ldB(wdown_s[l, mo + 2], B(f"wdown_s{l}")))
                act(yo[:, mo, 0:nt], o, AF.Copy, [B(f"ps_m{a_}")], [B("yo")])
                yield
            yield from g_post_res(nt, l, 5, v, xbuf, xname)

        def wstream(l, reps=1):
            one = [(wout_s[l, m], B(f"wout_s{l}")) for m in range(8)] + \
                  [(w[l, m], B(f"{nm}{l}")) for m in range(NFF) for (w, nm) in ((wgate_s, "wgate_s"), (wup_s, "wup_s"))]
            return Stream(one * reps)

        class WView:
            def __init__(self, st_, base):
                self.st_, self.base = st_, base

            def get(self, i):
                return self.st_.get(self.base + i)

        def drain(g):
            for _ in g:
                pass

        def interleave(ga, gt, ratio):
            a_alive, t_alive = ga is not None, gt is not None
            while a_alive or t_alive:
                if a_alive:
                    for _ in range(ratio):
                        try:
                            next(ga)
                        except StopIteration:
                            a_alive = False
                            break
                if t_alive:
                    try:
                        next(gt)
                    except StopIteration:
                        t_alive = False

        for l in range(L):
            last = (l == FL - 1)
            lp = l % 2
            X_in = xT_in if l == 0 else x_s
            X_out = out_T if l == L - 1 else x_s
            xin_b = (lambda c: B("xT_in")) if l == 0 else (lambda c: B(f"x_s{c}"))
            xout_b = (lambda c: B(f"outT{c}")) if l == L - 1 else (lambda c: B(f"x_s{c}"))

            P.dma("pool", wpool[:], wpool_in[l], reads=[], writes=[B("wpool")])

            norm_mod(xc, [B("xc")], CTX, l, 0, 1)
            in_proj(CTX, l, False, dict(
                a=aext[:, :, 8:8 + CTX], a_b=[B("aext")],
                qw=qwc, qw_b=[B("qwc")], kw=kwc[:, :], kw_b=[B("kwc")],
                qg=qgc, qg_b=[B("qgc")], kg=kgc[:, :], kg_b=[B("kgc")],
                vw=vwc, vw_b=[B("vwc")], vg=vgc, vg_b=[B("vgc")]), need_q=not last)
            if not last:
                P.op("dve", lambda: nc.vector.memset(aext[:, :, 0:8], 0.0), writes=[B("aext")])
                P.op("dve", lambda: nc.vector.memset(aext[:, :, 8 + CTX:16 + CTX], 0.0), writes=[B("aext")])
                drain(pool_mix(CTX, l, [(2, 0), (2, 1)]))
                for j in range(3):
                    tiles = [(kwc[:, i * 128:(i + 1) * 128], [B("kwc")], vwc[:, i], [B("vwc")], None, None) for i in range(2)]
                    drain(attend(qwc[:, j, :], [B("qwc")], CTX, tiles, j, yT[:, 2 + j, :], l))
                for j in range(3):
                    tiles = [(kgc[:, i * 128:(i + 1) * 128], [B("kgc")], vgc[:, i], [B("vgc")], None, None) for i in range(2)]
                    drain(attend(qgc[:, j, :], [B("qgc")], CTX, tiles, None, yT[:, 5 + j, :], l))
                wsx = wstream(l)
                drain(g_wout(CTX, l, wsx))
                drain(g_tail(CTX, l, 1, xc, "xc", wsx))

            def load_x(c):
                load(xa[:], X_in[:, :, c * T:(c + 1) * T].rearrange("k p t -> p k t"), [xin_b(c)], [B("xa")])

            load_x(0)
            for c in range(NCH):
                cs = slice(c * T, (c + 1) * T)
                load(ropeC[:], ropeC_in[:, cs], [], [B("ropeC")])
                load(ropeS[:], ropeS_in[:, cs], [], [B("ropeS")])
                norm_mod(xa, [B("xa")], T, l, 0, 0)
                if c + 1 < NCH:
                    load_x(c + 1)
                in_proj(T, l, True, dict(
                    a=ast, a_b=[B("ast")],
                    qw=qst[0], qw_b=[B("qst0")], kw=kst[0][:, :], kw_b=[B("kst0")],
                    qg=qst[1], qg_b=[B("qst1")], kg=kst[1][:, :], kg_b=[B("kst1")],
                    vw=vst[0], vw_b=[B("vst0")], vg=vst[1], vg_b=[B("vst1")]))
                dp = bass.ds(par_p, 1)
                store(a_t[lp][c][dp].rearrange("o p c t -> p (o c) t"), ast[:], [B("ast")], [B("d_a")])
                store(qw_s[lp, :, :, cs], qst[0][:], [B("qst0")], [B("d_qw")])
                store(qg_s[lp, :, :, cs], qst[1][:], [B("qst1")], [B("d_qg")])
                store(kw_t[lp][c][dp].rearrange("o p t -> p o t"), kst[0][:].rearrange("p (o t) -> p o t", o=1), [B("kst0")], [B("d_kw")])
                store(kg_t[lp][c][dp].rearrange("o p t -> p o t"), kst[1][:].rearrange("p (o t) -> p o t", o=1), [B("kst1")], [B("d_kg")])
                store(vw_t[lp][c][dp].rearrange("o s p f -> p (o s) f"),
                      vst[0][:].rearrange("p s k f -> p s (k f)"), [B("vst0")], [B("d_vw")])
                store(vg_t[lp][c][dp].rearrange("o s p f -> p (o s) f"),
                      vst[1][:].rearrange("p s k f -> p s (k f)"), [B("vst1")], [B("d_vg")])

            P.wait_bufs("pool", [B(n) for n in ("d_a", "d_qw", "d_qg", "d_kw", "d_kg", "d_vw", "d_vg")])
            nc.all_core_barrier()
            if l + 1 < L:
                cast_layer(l + 1)

            groups = [(hh, g) for hh in range(2) for g in range(4)]

            def issue_group(hh, g, s_):
                for u in range(2):
                    load(kgs[s_][:, u * T:(u + 1) * T], kg_t[lp][2 * g + u][hh], [], [B(f"kgs{s_}")])
                for u in range(2):
                    load(vgs[s_][:, 4 * u:4 * u + 4].rearrange("p s k f -> p s (k f)"), vg_t[lp][2 * g + u][hh].rearrange("s p f -> p s f"), [], [B(f"vgs{s_}")])

            def g_attn(c):
                cs = slice(c * T, (c + 1) * T)
                load(qst[0][:], qw_s[lp, :, :, cs], [], [B("qst0")])
                load(qst[1][:], qg_s[lp, :, :, cs], [], [B("qst1")])
                own = bass.ds(par, 1)
                oth = bass.ds(1 - par, 1)
                lo_sel, lo_c = (own, c - 1) if c > 0 else (oth, NCH - 1)
                hi_sel, hi_c = (own, c + 1) if c < NCH - 1 else (oth, 0)

                def k3(ap_):
                    return ap_.rearrange("p (o t) -> p o t", o=1)
                load(k3(kwin[:, 0:128]), kw_t[lp][lo_c][lo_sel, :, T - 128:T].rearrange("o p t -> p o t"), [], [B("kwin")])
                load(k3(kwin[:, 128:640]), kw_t[lp][c][own].rearrange("o p t -> p o t"), [], [B("kwin")])
                load(k3(kwin[:, 640:768]), kw_t[lp][hi_c][hi_sel, :, 0:128].rearrange("o p t -> p o t"), [], [B("kwin")])
                vw3 = vwin[:].rearrange("p s k f -> p s (k f)")
                load(vw3[:, 0:1], vw_t[lp][lo_c][lo_sel, 3:4].rearrange("o s p f -> p (o s) f"), [], [B("vwin")])
                load(vw3[:, 1:5], vw_t[lp][c][own].rearrange("o s p f -> p (o s) f"), [], [B("vwin")])
                load(vw3[:, 5:6], vw_t[lp][hi_c][hi_sel, 0:1].rearrange("o s p f -> p (o s) f"), [], [B("vwin")])
                load(aext[:, :, 0:8], a_t[lp][lo_c][lo_sel, :, :, T - 8:T].rearrange("o p c t -> p (o c) t"), [], [B("aext")])
                load(aext[:, :, 8:8 + T], a_t[lp][c][own].rearrange("o p c t -> p (o c) t"), [], [B("aext")])
                load(aext[:, :, 8 + T:16 + T], a_t[lp][hi_c][hi_sel, :, :, 0:8].rearrange("o p c t -> p (o c) t"), [], [B("aext")])
                if c == 0:
                    issue_group(groups[0][0], groups[0][1], 0)
                yield

                def g_pool():
                    edges = []
                    if c == 0:
                        tsm(aext[:, :, 0:8], aext[:, :, 0:8], halo[:, 0:1], [B("aext"), B("halo")], [B("aext")])
                        edges.append((0, 0))
                    if c == NCH - 1:
                        tsm(aext[:, :, 8 + T:16 + T], aext[:, :, 8 + T:16 + T], halo[:, 1:2], [B("aext"), B("halo")], [B("aext")])
                        edges.append((1, 1))
                    yield from pool_mix(T, l, edges)
                for j in range(3):
                    tiles = []
                    for r in range(6):
                        if r == 0:
                            mk = masks[:, 6 if c == 0 else 0, :]
                        elif r == 5:
                            mk = masks[:, 7 if c == NCH - 1 else 5, :]
                        else:
                            mk = masks[:, r, :]
                        tiles.append((kwin[:, r * 128:(r + 1) * 128], [B("kwin")], vwin[:, r], [B("vwin")], mk, None))
                    for i in range(2):
                        tiles.append((kwc[:, i * 128:(i + 1) * 128], [B("kwc")], vwc[:, i], [B("vwc")], None, None))
                    yield from attend(qst[0][:, j, :], [B("qst0")], T, tiles, j, yT[:, 2 + j, :], l)
                for j in range(3):
                    tiles = [(kgc[:, i * 128:(i + 1) * 128], [B("kgc")], vgc[:, i], [B("vgc")], None, None) for i in range(2)]
                    for gi, (hh, g) in enumerate(groups):
                        s_ = gi % 2
                        if gi + 1 < len(groups):
                            nh = groups[gi + 1]
                        elif j + 1 < 3 or c + 1 < NCH:
                            nh = groups[0]
                        else:
                            nh = None
                        for i in range(8):
                            pre = None
                            if i == 2 and nh is not None:
                                pre = (lambda nh=nh, ns=(gi + 1) % 2: issue_group(nh[0], nh[1], ns))
                            tiles.append((kgs[s_][:, i * 128:(i + 1) * 128], [B(f"kgs{s_}")], vgs[s_][:, i], [B(f"vgs{s_}")], None, pre))
                    if j == 2:
                        yield from g_pool()
                    yield from attend(qst[1][:, j, :], [B("qst1")], T, tiles, None, yT[:, 5 + j, :], l)
                return
                load(aext[:, :, 0:8], a_t[lp][lo_c][lo_sel, :, :, T - 8:T].rearrange("o p c t -> p (o c) t"), [], [B("aext")])
                load(aext[:, :, 8:8 + T], a_t[lp][c][own].rearrange("o p c t -> p (o c) t"), [], [B("aext")])
                load(aext[:, :, 8 + T:16 + T], a_t[lp][hi_c][hi_sel, :, :, 0:8].rearrange("o p c t -> p (o c) t"), [], [B("aext")])
                yield
                edges = []
                if c == 0:
                    tsm(aext[:, :, 0:8], aext[:, :, 0:8], halo[:, 0:1], [B("aext"), B("halo")], [B("aext")])
                    edges.append((0, 0))
                if c == NCH - 1:
                    tsm(aext[:, :, 8 + T:16 + T], aext[:, :, 8 + T:16 + T], halo[:, 1:2], [B("aext"), B("halo")], [B("aext")])
                    edges.append((1, 1))
                yield from pool_mix(T, l, edges)

            drain(g_attn(0))
            wsl = wstream(l, NCH)
            for c in range(NCH):
                cs = slice(c * T, (c + 1) * T)
                ws = WView(wsl, c * (8 + 2 * NFF))
                drain(g_wout(T, l, ws))
                ga = g_attn(c + 1) if c + 1 < NCH else None
                if ga is not None:
                    for _ in range(2):
                        next(ga)
                load(xa[:], X_in[:, :, cs].rearrange("k p t -> p k t"), [xin_b(c)], [B("xa")])
                interleave(ga, g_tail(T, l, 0, xa, "xa", ws), RATIO)
                store(X_out[:, :, cs].rearrange("k p t -> p k t"), xa[:], [B("xa")], [xout_b(c)])

        P.wait_bufs("pool", [B(f"outT{c}") for c in range(NCH)])
    return nc


def _pair_cols(base):
    return [np.r_[base + j * 64: base + j * 64 + 64, base + (3 + j) * 64: base + (3 + j) * 64 + 64] for j in range(3)]


def _cnt(t, w, n):
    lo = np.clip(t - w // 2, 0, n - 1)
    hi = np.clip(t + (w - 1 - w // 2), 0, n - 1)
    return (hi - lo + 1).astype(np.float32)


def _prep_inputs(inp, L):
    f32 = np.float32
    sh = {}
    in_chunks = [np.arange(0, 128), np.arange(128, 256)] + _pair_cols(256) + [np.arange(640, 768), np.arange(768, 896)] \
        + _pair_cols(896) + [np.arange(1280, 1408), np.arange(1408, 1536)]
    cols = np.concatenate(in_chunks)
    w = np.asarray(inp["w_in"], f32)[:L][:, :, cols]
    sh["w_in"] = np.ascontiguousarray(w.reshape(L, 8, 128, 12, 128).transpose(0, 3, 2, 1, 4))
    rows = np.concatenate([np.arange(0, 128), np.arange(128, 256)] + _pair_cols(256) + _pair_cols(640))
    w = np.asarray(inp["w_out"], f32)[:L][:, rows, :]
    sh["w_out"] = np.ascontiguousarray(w.reshape(L, 8, 128, 8, 128).transpose(0, 3, 2, 1, 4))
    for k in ("w_gate", "w_up"):
        w = np.asarray(inp[k], f32)[:L]
        sh[k] = np.ascontiguousarray(w.reshape(L, 8, 128, NFF, 128).transpose(0, 3, 2, 1, 4))
    w = np.asarray(inp["w_down"], f32)[:L]
    sh["w_down"] = np.ascontiguousarray(w.reshape(L, NFF, 128, 8, 128).transpose(0, 3, 2, 1, 4))
    w = np.asarray(inp["w_mod"], f32)[:L]
    sh["wmod"] = np.ascontiguousarray(w.reshape(L, 8, 128, 12, 512).transpose(0, 3, 2, 1, 4))
    b = np.asarray(inp["b_mod"], f32)[:L]
    sh["bmod"] = np.ascontiguousarray(np.stack([b, b], axis=1))
    g = np.stack([np.asarray(inp[k], f32)[:L] for k in ("g_pre_mix", "g_post_mix", "g_pre_ffn", "g_post_ffn")], axis=1)
    sh["gains"] = np.ascontiguousarray(g.reshape(L, 4, 8, 128).transpose(3, 0, 1, 2))
    ps = np.asarray(inp["pool_scale"], f32)[:L]
    sh["pscale"] = np.ascontiguousarray(ps.reshape(L, 2, 128).transpose(2, 0, 1))
    ws = np.asarray(inp["win_sink"], f32)[:L]
    sk = np.zeros((128, L, 3), f32)
    sk[0:64] = ws[None, :, 0:3]
    sk[64:128] = ws[None, :, 3:6]
    sh["sink"] = sk
    gq = np.asarray(inp["g_qnorm"], f32)[:L]
    gk = np.asarray(inp["g_knorm"], f32)[:L]
    gqk = np.zeros((128, L, 2), f32)
    gqk[:, :, 0] = np.concatenate([gq, gq], axis=1).T
    gqk[:, :, 1] = np.concatenate([gk, gk], axis=1).T
    sh["gqk"] = gqk
    wp = np.asarray(inp["w_pool"], f32)[:L]
    wpb = np.zeros((L, 128, 2, 128), f32)
    for c in range(2):
        for gl in range(2):
            wpb[:, gl * 64:(gl + 1) * 64, c, gl * 64:(gl + 1) * 64] = wp[:, 2 * c + gl]
    sh["w_pool"] = wpb
    consts = np.zeros((128, 3, 128), f32)
    for m in range(128):
        d = m % 64
        half = (d % 32) // 16
        partner = m + 16 if half == 0 else m - 16
        consts[partner, 0, m] = 1.0
        consts[(m + 64) % 128, 1, m] = 1.0
    consts[0, 2, 0] = 1.0
    consts[1, 2, 1] = 1.0
    sh["consts"] = consts
    cbf = np.zeros((128, 3, 128), f32)
    cbf[:, 0, :] = 1.0
    cbf[0:64, 1, 0:64] = 1.0
    cbf[64:128, 1, 64:128] = 1.0
    cbf[np.arange(128), 2, np.arange(128)] = 1.0
    sh["cbf"] = cbf.astype(ml_dtypes.bfloat16)
    jj = np.arange(128)[:, None]
    qq = np.arange(T)[None, :]
    NEGM = np.float32(-30000.0)
    base_masks = np.stack([np.where(np.abs(128 * (r - 1) + jj - qq) <= 128, np.float32(0.0), NEGM) for r in range(6)], axis=1).astype(f32)
    freq = (np.float32(10000.0) ** (-np.arange(16, dtype=f32) / np.float32(16))).astype(f32)
    x = np.asarray(inp["x"], f32)
    ctx = np.asarray(inp["ctx"], f32)
    cvec = np.asarray(inp["c"], f32)
    cctx = np.asarray(inp["c_ctx"], f32)
    per_core = []
    pwin = (2, 4, 8, 16)
    for r in range(8):
        b_, h_ = r // 2, r % 2
        m = dict(sh)
        xs = x[b_, h_ * HALF:(h_ + 1) * HALF, :]
        m["xT"] = np.ascontiguousarray(xs.T).reshape(8, 128, HALF)
        m["xcT"] = np.ascontiguousarray(ctx[b_].T.reshape(8, 128, CTX).transpose(1, 0, 2))
        cc = np.stack([cvec[b_], cctx], axis=1)
        m["ccT"] = np.ascontiguousarray(cc.reshape(8, 128, 2).transpose(1, 0, 2))
        tg = (h_ * HALF + np.arange(HALF)).astype(f32)
        rowp = np.floor(tg / 64).astype(f32)
        colp = (tg - rowp * 64).astype(f32)
        C = np.zeros((128, HALF), f32)
        S = np.zeros((128, HALF), f32)
        for p in range(128):
            d = p % 64
            axis = d // 32
            half = (d % 32) // 16
            f = d % 16
            ang = ((rowp if axis == 0 else colp) * freq[f]).astype(f32)
            C[p] = np.cos(ang)
            S[p] = np.sin(ang) * (-1.0 if half == 0 else 1.0)
        m["ropeC"] = C
        m["ropeS"] = S
        mk = np.full((128, 8, T), NEGM, f32)
        mk[:, 0:6] = base_masks
        if h_ == 1:
            mk[:, 6] = base_masks[:, 0]
        if h_ == 0:
            mk[:, 7] = base_masks[:, 5]
        m["masks"] = mk.astype(ml_dtypes.bfloat16)
        ic = np.zeros((128, 3, 2, 16), f32)
        for p in range(128):
            for c in range(2):
                w_ = pwin[2 * c + p // 64]
                ic[p, :, c, :] = 1.0 / w_
                t0 = h_ * HALF + np.arange(8)
                ic[p, 0, c, 0:8] = 1.0 / _cnt(t0, w_, SEQ)
                t1 = h_ * HALF + HALF - 8 + np.arange(8)
                ic[p, 1, c, 8:16] = 1.0 / _cnt(t1, w_, SEQ)
                ic[p, 2, c, 0:8] = 1.0 / _cnt(np.arange(8), w_, CTX)
                ic[p, 2, c, 8:16] = 1.0 / _cnt(CTX - 8 + np.arange(8), w_, CTX)
        m["icnt"] = ic
        hl = np.zeros((128, 2), f32)
        hl[:, 0] = 1.0 if h_ == 1 else 0.0
        hl[:, 1] = 1.0 if h_ == 0 else 0.0
        m["halo"] = hl
        per_core.append(m)
    return per_core


_NC_CACHE = {}


def kernel(n_layers=DEPTH, **inputs):
    L = n_layers
    if L not in _NC_CACHE:
        _NC_CACHE[L] = build_program(L)
    nc = _NC_CACHE[L]
    in_maps = _prep_inputs(inputs, L)
    res = run_bass_kernel_spmd(nc, in_maps, core_ids=list(range(8)))
    out = np.empty((BATCH, SEQ, D), np.float32)
    for r in range(8):
        b_, h_ = r // 2, r % 2
        o = np.asarray(res.results[r]["outT"], np.float32).reshape(D, HALF)
        out[b_, h_ * HALF:(h_ + 1) * HALF, :] = o.T
    return out
```

```python
from contextlib import ExitStack
import numpy as np
import ml_dtypes
import concourse.bass as bass
import concourse.mybir as mybir
from concourse.bass_utils import run_bass_kernel_spmd

F32 = mybir.dt.float32
BF16 = mybir.dt.bfloat16
AF = mybir.ActivationFunctionType
ALU = mybir.AluOpType

D = 1024
SEQ = 8192
BATCH = 4
DEPTH = 4
CTX = 256
HALF = SEQ // 2
T = 512
NCH = HALF // T
DFF = 2816
NFF = DFF // 128
EPS = 1e-6
NMOD = 6


class Buf:
    __slots__ = ("name", "wtok", "rtoks", "dsem", "dcount")

    def __init__(self, name):
        self.name = name
        self.wtok = None
        self.rtoks = {}
        self.dsem = None
        self.dcount = 0


class Prog:
    def __init__(self, nc, stack):
        self.nc = nc
        self.stack = stack
        self.engs = {"pe": nc.tensor, "act": nc.scalar, "dve": nc.vector, "pool": nc.gpsimd, "sp": nc.sync}
        self.sems = {}
        self.count = {}
        for e in self.engs:
            self.sems[e] = stack.enter_context(nc.semaphore("s_" + e))
            self.count[e] = 0
        self.waited = {e: {} for e in self.engs}
        self.ninstr = 0

    def _need(self, eng, tok):
        if tok is None:
            return
        key, val = tok
        if eng == "pe" and key == "pe":
            return
        w = self.waited[eng]
        if w.get(key, 0) >= val:
            return
        w[key] = val
        self.engs[eng].wait_ge(self.sems[key], val)

    def _deps(self, eng, reads, writes, skip_key=None):
        for b in reads:
            self._need(eng, b.wtok)
        for b in writes:
            if b.wtok is not None and b.wtok[0] != skip_key:
                self._need(eng, b.wtok)
            for k, v in b.rtoks.items():
                self._need(eng, (k, v))

    @staticmethod
    def _commit(tok, reads, writes):
        k, v = tok
        for b in reads:
            if b.rtoks.get(k, 0) < v:
                b.rtoks[k] = v
        for b in writes:
            b.wtok = tok
            b.rtoks = {}

    def op(self, eng, fn, reads=(), writes=(), inc=True):
        self._deps(eng, reads, writes)
        ins = fn()
        self.ninstr += 1
        if inc:
            self.count[eng] += 1
            ins.then_inc(self.sems[eng], 1)
            tok = (eng, self.count[eng])
        else:
            tok = (eng, self.count[eng] + 1)
        self._commit(tok, reads, writes)
        return tok

    def dma(self, eng, out, in_, reads=(), writes=(), **kw):
        assert len(writes) == 1
        wb = writes[0]
        if wb.dsem is None:
            key = "d_" + wb.name
            self.sems[key] = self.stack.enter_context(self.nc.semaphore(key))
            wb.dsem = key
        self._deps(eng, reads, writes, skip_key=wb.dsem)
        wb.dcount += 16
        ins = self.engs[eng].dma_start(out=out, in_=in_, **kw)
        ins.then_inc(self.sems[wb.dsem], 16)
        self.ninstr += 1
        tok = (wb.dsem, wb.dcount)
        rt = wb.rtoks
        self._commit(tok, reads, writes)
        return tok

    def wait_bufs(self, eng, bufs):
        for b in bufs:
            if b.wtok is not None:
                key, val = b.wtok
                w = self.waited[eng]
                if w.get(key, 0) < val:
                    w[key] = val
                    self.engs[eng].wait_ge(self.sems[key], val)


RATIO = 3


def build_program(n_layers=DEPTH, final_layers=None):
    nc = bass.Bass("TRN2", target_bir_lowering=False, num_devices=8)
    L = n_layers
    FL = L if final_layers is None else final_layers

    def din(name, shape, dt=F32):
        return nc.dram_tensor(name, list(shape), dt, kind="ExternalInput").ap()

    xT_in = din("xT", [8, 128, HALF])
    xcT_in = din("xcT", [128, 8, CTX])
    ccT_in = din("ccT", [128, 8, 2])
    wmod_in = din("wmod", [L, 12, 128, 8, 512])
    bmod_in = din("bmod", [L, 2, 6144])
    gains_in = din("gains", [128, L, 4, 8])
    pscale_in = din("pscale", [128, L, 2])
    sink_in = din("sink", [128, L, 3])
    gqk_in = din("gqk", [128, L, 2])
    win_in = din("w_in", [L, 12, 128, 8, 128])
    wout_in = din("w_out", [L, 8, 128, 8, 128])
    wgate_in = din("w_gate", [L, NFF, 128, 8, 128])
    wup_in = din("w_up", [L, NFF, 128, 8, 128])
    wdown_in = din("w_down", [L, 8, 128, NFF, 128])
    wpool_in = din("w_pool", [L, 128, 2, 128])
    ropeC_in = din("ropeC", [128, HALF])
    ropeS_in = din("ropeS", [128, HALF])
    consts_in = din("consts", [128, 3, 128])
    cbf_in = din("cbf", [128, 3, 128], BF16)
    masks_in = din("masks", [128, 8, T], BF16)
    icnt_in = din("icnt", [128, 3, 2, 16])
    halo_in = din("halo", [128, 2])
    out_T = nc.dram_tensor("outT", [8, 128, HALF], F32, kind="ExternalOutput").ap()

    def dint(name, shape, dt, shared=False):
        if shared:
            return nc.dram_tensor(name, list(shape), dt, addr_space="Shared").ap()
        return nc.dram_tensor(name, list(shape), dt).ap()

    x_s = dint("x_s", [8, 128, HALF], F32)
    win_s = dint("win_s", [L, 12, 128, 8, 128], BF16)
    wout_s = dint("wout_s", [L, 8, 128, 8, 128], BF16)
    wgate_s = dint("wgate_s", [L, NFF, 128, 8, 128], BF16)
    wup_s = dint("wup_s", [L, NFF, 128, 8, 128], BF16)
    wdown_s = dint("wdown_s", [L, 8, 128, NFF, 128], BF16)
    qw_s = dint("qw_s", [2, 128, 3, HALF], BF16)
    qg_s = dint("qg_s", [2, 128, 3, HALF], BF16)
    kw_t = [[dint(f"kw_sh{p}_{c}", [2, 128, T], BF16, True) for c in range(NCH)] for p in range(2)]
    kg_t = [[dint(f"kg_sh{p}_{c}", [2, 128, T], BF16, True) for c in range(NCH)] for p in range(2)]
    vw_t = [[dint(f"vw_sh{p}_{c}", [2, 4, 128, 256], BF16, True) for c in range(NCH)] for p in range(2)]
    vg_t = [[dint(f"vg_sh{p}_{c}", [2, 4, 128, 256], BF16, True) for c in range(NCH)] for p in range(2)]
    a_t = [[dint(f"a_sh{p}_{c}", [2, 128, 2, T], F32, True) for c in range(NCH)] for p in range(2)]

    st = ExitStack()
    with st:
        P = Prog(nc, st)
        bufs = {}

        def B(name):
            if name not in bufs:
                bufs[name] = Buf(name)
            return bufs[name]

        def sb(name, shape, dt):
            return st.enter_context(nc.sbuf_tensor("sb_" + name, list(shape), dt))

        def psum(name, shape):
            return st.enter_context(nc.psum_tensor(name, list(shape), F32))

        par = nc.sync.partition_id() % 2
        par_p = nc.gpsimd.partition_id() % 2

        consts = sb("consts", [128, 3, 128], F32)
        cbf = sb("cbf", [128, 3, 128], BF16)
        masks = sb("masks", [128, 8, T], BF16)
        icnt = sb("icnt", [128, 3, 2, 16], F32)
        halo = sb("halo", [128, 2], F32)
        gains = sb("gains", [128, L, 4, 8], F32)
        pscale = sb("pscale", [128, L, 2], F32)
        esink = sb("esink", [128, L, 3], F32)
        gqk = sb("gqk", [128, L, 2], F32)
        epsc = sb("epsc", [128, 1], F32)
        modT = sb("modT", [128, L, 48, 2], F32)
        tabs = sb("tabs", [128, L, 6, 8, 2], F32)
        ccT = sb("ccT", [128, 8, 2], F32)
        scc = sb("scc", [128, 8, 2], F32)
        sccb = sb("sccb", [128, 8, 2], BF16)
        xc = sb("xc", [128, 8, CTX], F32)
        wpool = sb("wpool", [128, 2, 128], BF16)
        kwc = sb("kwc", [128, CTX], BF16)
        kgc = sb("kgc", [128, CTX], BF16)
        vwc = sb("vwc", [128, 2, 2, 128], BF16)
        vgc = sb("vgc", [128, 2, 2, 128], BF16)
        qwc = sb("qwc", [128, 3, CTX], BF16)
        qgc = sb("qgc", [128, 3, CTX], BF16)
        aext = sb("aext", [128, 2, T + 16], F32)
        xa = sb("xa", [128, 8, T], F32)
        hT = sb("hT", [128, 8, T], BF16)
        rstd = sb("rstd", [128, T], F32)
        tmp = [sb(f"tmp{i}", [128, T], F32) for i in range(4)]
        qf = [sb(f"qf{i}", [128, T], F32) for i in range(2)]
        sqh = sb("sqh", [128, T], BF16)
        ropeC = sb("ropeC", [128, T], F32)
        ropeS = sb("ropeS", [128, T], F32)
        ringA = [sb(f"ringA{i}", [128, 8, 128], BF16) for i in range(8)]
        ringB = [sb(f"ringB{i}", [128, NFF, 128], BF16) for i in range(2)]
        qst = [sb(f"qst{i}", [128, 3, T], BF16) for i in range(2)]
        kst = [sb(f"kst{i}", [128, T], BF16) for i in range(2)]
        vst = [sb(f"vst{i}", [128, 4, 2, 128], BF16) for i in range(2)]
        ast = sb("ast", [128, 2, T], F32)
        kwin = sb("kwin", [128, 6 * 128], BF16)
        vwin = sb("vwin", [128, 6, 2, 128], BF16)
        kgs = [sb(f"kgs{i}", [128, 1024], BF16) for i in range(2)]
        vgs = [sb(f"vgs{i}", [128, 8, 2, 128], BF16) for i in range(2)]
        PT = [sb(f"PT{i}", [128, 2 * T], BF16) for i in range(3)]
        dsb = sb("dsb", [128, T], F32)
        rec = sb("rec", [128, T], F32)
        yT = sb("yT", [128, 8, T], BF16)
        yo = sb("yo", [128, 8, T], F32)
        actT = sb("actT", [128, NFF, T], BF16)
        sg = [sb(f"sg{i}", [128, T], F32) for i in range(2)]
        feat = sb("feat", [128, 2, T], BF16)
        sq = actT[:, 0:8, :]
        yo_flat = yo[:].rearrange("p k t -> p (k t)")
        pwt = [sb(f"pwt{i}", [128, T + 16], F32) for i in range(3)]

        ps_s = [psum(f"ps_s{i}", [128, 2 * T]) for i in range(2)]
        ps_acc = [psum(f"ps_acc{i}", [128, T]) for i in range(2)]
        ps_m = [psum(f"ps_m{i}", [128, T]) for i in range(2)]

        def mm(out, lhsT, rhs, start, stop, reads, writes, inc):
            return P.op("pe", lambda: nc.tensor.matmul(out, lhsT, rhs, start=start, stop=stop),
                        reads=reads, writes=writes, inc=inc)

        def act(out, in_, func, reads, writes, bias=None, scale=None):
            kw = {}
            if bias is not None:
                kw["bias"] = bias
            if scale is not None:
                kw["scale"] = scale
            return P.op("act", lambda: nc.scalar.activation(out=out, in_=in_, func=func, **kw), reads=reads, writes=writes)

        def tt(out, in0, in1, op, reads, writes):
            return P.op("dve", lambda: nc.vector.tensor_tensor(out, in0, in1, op), reads=reads, writes=writes)

        def stt(out, in0, scalar, in1, op0, op1, reads, writes):
            return P.op("dve", lambda: nc.vector.scalar_tensor_tensor(out, in0, scalar, in1, op0, op1), reads=reads, writes=writes)

        def tsm(out, in0, s1, reads, writes, op=ALU.mult):
            return P.op("dve", lambda: nc.vector.tensor_scalar(out, in0, s1, None, op), reads=reads, writes=writes)

        def recip(out, in_, reads, writes):
            return P.op("dve", lambda: nc.vector.reciprocal(out, in_), reads=reads, writes=writes)

        def vcopy(out, in_, reads, writes):
            return P.op("dve", lambda: nc.vector.tensor_copy(out, in_), reads=reads, writes=writes)

        def load(out, in_, reads, writes):
            return P.dma("sp", out, in_, reads=reads, writes=writes)

        def store(out, in_, reads, writes):
            return P.dma("pool", out, in_, reads=reads, writes=writes)

        load(consts[:], consts_in, [], [B("consts")])
        load(cbf[:], cbf_in, [], [B("cbf")])
        load(masks[:], masks_in, [], [B("masks")])
        load(icnt[:], icnt_in, [], [B("icnt")])
        load(halo[:], halo_in, [], [B("halo")])
        load(gains[:], gains_in, [], [B("gains")])
        load(pscale[:], pscale_in, [], [B("pscale")])
        load(esink[:], sink_in, [], [B("esink")])
        load(gqk[:], gqk_in, [], [B("gqk")])
        load(ccT[:], ccT_in, [], [B("ccT")])
        load(xc[:], xcT_in, [], [B("xc")])
        P.op("dve", lambda: nc.vector.memset(epsc[:], EPS), writes=[B("epsc")])
        for i in range(2):
            P.op("dve", lambda i=i: nc.vector.memset(vst[i][:], 1.0), writes=[B(f"vst{i}")])
        P.op("dve", lambda: nc.vector.memset(vwc[:], 1.0), writes=[B("vwc")])
        P.op("dve", lambda: nc.vector.memset(vgc[:], 1.0), writes=[B("vgc")])
        act(esink[:], esink[:], AF.Exp, [B("esink")], [B("esink")])
        act(scc[:], ccT[:], AF.Exp, [B("ccT")], [B("scc")], scale=-1.0)
        tsm(scc[:], scc[:], 1.0, [B("scc")], [B("scc")], op=ALU.add)
        recip(scc[:], scc[:], [B("scc")], [B("scc")])
        tt(scc[:], ccT[:], scc[:], ALU.mult, [B("ccT"), B("scc")], [B("scc")])
        vcopy(sccb[:], scc[:], [B("scc")], [B("sccb")])
        perm = consts[:, 0, :]
        swap = consts[:, 1, :]
        eye2 = consts[0:2, 2, 0:2]
        ones_bf = cbf[:, 0, :]
        bd_bf = cbf[:, 1, :]
        ident_bf = cbf[:, 2, :]

        cast_toks = []

        def cast_layer(l):
            def cast(dst, src, name, nsplit):
                n0 = dst.shape[0]
                step = max(1, n0 // nsplit)
                for i in range(0, n0, step):
                    if len(cast_toks) >= 2:
                        P._need("pool", cast_toks[-2])
                    cast_toks.append(P.dma("pool", dst[i:i + step], src[i:i + step], reads=[], writes=[B(name)]))
            cast(win_s[l], win_in[l], f"win_s{l}", 4)
            cast(wout_s[l], wout_in[l], f"wout_s{l}", 2)
            cast(wgate_s[l], wgate_in[l], f"wgate_s{l}", 11)
            cast(wup_s[l], wup_in[l], f"wup_s{l}", 11)
            cast(wdown_s[l], wdown_in[l], f"wdown_s{l}", 8)

        cast_layer(0)

        mrows = [tmp[0], tmp[1], tmp[2], tmp[3]]
        brows = [qf[0], qf[1], sg[0], sg[1]]
        mrn = ["tmp0", "tmp1", "tmp2", "tmp3"]
        brn = ["qf0", "qf1", "sg0", "sg1"]
        mst = [(xa, "xa"), (yo, "yo")]
        msi = 0
        for l in range(L):
            for g3 in range(3):
                for n4 in range(4):
                    n = g3 * 4 + n4
                    load(brows[n4][0:2, :], bmod_in[l, :, n * 512:(n + 1) * 512], [], [B(brn[n4])])
                    ms_, msn = mst[msi % 2]
                    msi += 1
                    load(ms_[:], wmod_in[l, n], [], [B(msn)])
                    for k in range(8):
                        mm(ps_m[0][0:2, :], scc[:, k, :], ms_[:, k, :], k == 0, k == 7,
                           [B("scc"), B(msn)], [B("ps_m0")], k == 7)
                    tt(mrows[n4][0:2, :], ps_m[0][0:2, :], brows[n4][0:2, :], ALU.add,
                       [B("ps_m0"), B(brn[n4])], [B(mrn[n4])])
                for jj in range(16):
                    j = g3 * 16 + jj
                    mm(ps_m[1][:, 2 * j:2 * j + 2], mrows[jj // 4][0:2, (jj % 4) * 128:(jj % 4 + 1) * 128], eye2, True, True,
                       [B(mrn[jj // 4]), B("consts")], [B("ps_m1")], True)
            vcopy(modT[:, l, :, :], ps_m[1][:, 0:96].rearrange("p (j v) -> p j v", v=2), [B("ps_m1")], [B("modT")])
            for v in range(2):
                for s, (gi_pre, gi_post) in enumerate(((0, 1), (2, 3))):
                    m0 = 3 * s
                    stt(tabs[:, l, 3 * s + 0, :, v], modT[:, l, (m0 + 1) * 8:(m0 + 2) * 8, v], 1.0, gains[:, l, gi_pre, :],
                        ALU.add, ALU.mult, [B("modT"), B("gains")], [B("tabs")])
                    vcopy(tabs[:, l, 3 * s + 1, :, v], modT[:, l, m0 * 8:(m0 + 1) * 8, v], [B("modT")], [B("tabs")])
                    tt(tabs[:, l, 3 * s + 2, :, v], modT[:, l, (m0 + 2) * 8:(m0 + 3) * 8, v], gains[:, l, gi_post, :], ALU.mult,
                       [B("modT"), B("gains")], [B("tabs")])

        def rms_stats(src_ap, src_bufs, nt, width):
            act(sq[:, :, 0:nt], src_ap, AF.Square, src_bufs, [B("actT")])
            for k in range(8):
                mm(ps_m[0][:, 0:nt], ones_bf, sq[:, k, 0:nt], k == 0, k == 7, [B("cbf"), B("actT")], [B("ps_m0")], k == 7)
            act(rstd[:, 0:nt], ps_m[0][:, 0:nt], AF.Ln, [B("ps_m0"), B("epsc")], [B("rstd")], bias=epsc[:, 0:1], scale=1.0 / width)
            act(rstd[:, 0:nt], rstd[:, 0:nt], AF.Exp, [B("rstd")], [B("rstd")], scale=-0.5)

        def norm_mod(src, src_bufs, nt, l, ti, v):
            rms_stats(src[:, :, 0:nt], src_bufs, nt, D)
            for k in range(8):
                t = tmp[k % 2]
                stt(t[:, 0:nt], src[:, k, 0:nt], tabs[:, l, ti, k, v:v + 1], rstd[:, 0:nt], ALU.mult, ALU.mult,
                    src_bufs + [B("tabs"), B("rstd")], [B(f"tmp{k % 2}")])
                act(hT[:, k, 0:nt], t[:, 0:nt], AF.Identity, [B(f"tmp{k % 2}"), B("tabs")], [B("hT")],
                    bias=tabs[:, l, ti + 1, k, v:v + 1], scale=1.0)

        def post_res(nt, l, gi, v, xbuf, xname):
            rms_stats(yo[:, :, 0:nt], [B("yo")], nt, D)
            for k in range(8):
                t = tmp[k % 2]
                tt(t[:, 0:nt], yo[:, k, 0:nt], rstd[:, 0:nt], ALU.mult, [B("yo"), B("rstd")], [B(f"tmp{k % 2}")])
                stt(xbuf[:, k, 0:nt], t[:, 0:nt], tabs[:, l, gi, k, v:v + 1], xbuf[:, k, 0:nt], ALU.mult, ALU.add,
                    [B(f"tmp{k % 2}"), B("tabs"), B(xname)], [B(xname)])

        ra = [0]
        rb = [0]

        def ldA(src_ap, srcb):
            i = ra[0] % 8
            ra[0] += 1
            load(ringA[i][:], src_ap, [srcb], [B(f"ringA{i}")])
            return ringA[i], B(f"ringA{i}")

        def ldB(src_ap, srcb):
            i = rb[0] % 2
            rb[0] += 1
            load(ringB[i][:], src_ap, [srcb], [B(f"ringB{i}")])
            return ringB[i], B(f"ringB{i}")

        class Stream:
            def __init__(self, seq, ahead=7):
                self.seq = seq
                self.loaded = []
                self.ahead = ahead

            def get(self, i):
                while len(self.loaded) < min(len(self.seq), i + self.ahead):
                    self.loaded.append(ldA(*self.seq[len(self.loaded)]))
                return self.loaded[i]

        def qk_post(psb, psname, nt, l, is_glb, gcol, rope, dst, dst_bufs, slot):
            if slot == 0:
                f, fb = qf[0], B("qf0")
                sq_, sqb = sqh, B("sqh")
                rs_, rsb = rstd, B("rstd")
                pa, pab = ps_m[0], B("ps_m0")
                pb_, pbb = ps_m[1], B("ps_m1")
                t2, t2b = tmp[2], B("tmp2")
                t3, t3b = tmp[3], B("tmp3")
            else:
                f, fb = qf[1], B("qf1")
                sq_, sqb = PT[0], B("PT0")
                rs_, rsb = rec, B("rec")
                pa, pab = ps_acc[0], B("ps_acc0_0")
                pb_, pbb = ps_acc[1], B("ps_acc1_0")
                t2, t2b = dsb, B("dsb")
                t3, t3b = sg[0], B("sg0")
            if is_glb:
                act(sq_[:, 0:nt], psb[:, 0:nt], AF.Square, [B(psname)], [sqb])
                mm(pa[:, 0:nt], bd_bf, sq_[:, 0:nt], True, True, [B("cbf"), sqb], [pab], True)
                act(rs_[:, 0:nt], pa[:, 0:nt], AF.Ln, [pab, B("epsc")], [rsb], bias=epsc[:, 0:1], scale=1.0 / 64)
                act(rs_[:, 0:nt], rs_[:, 0:nt], AF.Exp, [rsb], [rsb], scale=-0.5)
                stt(f[:, 0:nt] if rope else dst, psb[:, 0:nt], gqk[:, l, gcol:gcol + 1], rs_[:, 0:nt], ALU.mult, ALU.mult,
                    [B(psname), B("gqk"), rsb], [fb] if rope else dst_bufs)
            else:
                act(f[:, 0:nt] if rope else dst, psb[:, 0:nt], AF.Copy, [B(psname)], [fb] if rope else dst_bufs)
            if rope:
                mm(pb_[:, 0:nt], perm, f[:, 0:nt], True, True, [B("consts"), fb], [pbb], True)
                tt(t2[:, 0:nt], f[:, 0:nt], ropeC[:, 0:nt], ALU.mult, [fb, B("ropeC")], [t2b])
                tt(t3[:, 0:nt], pb_[:, 0:nt], ropeS[:, 0:nt], ALU.mult, [pbb, B("ropeS")], [t3b])
                tt(dst, t2[:, 0:nt], t3[:, 0:nt], ALU.add, [t2b, t3b], dst_bufs)

        PSROT = [(ps_s[0], "ps_s0", 0), (ps_s[0], "ps_s0", 1), (ps_s[1], "ps_s1", 0), (ps_s[1], "ps_s1", 1)]

        def in_proj(nt, l, latent, dsts, need_q=True):
            ws = Stream([(win_s[l, mc], B(f"win_s{l}")) for mc in range(12)])
            rot = [0]

            def proj_fm(mc):
                t_, name, hf = PSROT[rot[0] % len(PSROT)]
                rot[0] += 1
                o = t_[:, hf * T:hf * T + nt]
                w_, wb = ws.get(mc)
                for k in range(8):
                    mm(o, w_[:, k, :], hT[:, k, 0:nt], k == 0, k == 7, [wb, B("hT")], [B(name + f"_{hf}")], k == 7)
                return t_[:, hf * T:(hf + 1) * T], name + f"_{hf}"

            def proj_v(mc, key):
                t_, name, hf = PSROT[rot[0] % len(PSROT)]
                rot[0] += 1
                w_, wb = ws.get(mc)
                nsub = nt // 128
                for s_ in range(nsub):
                    for k in range(8):
                        mm(t_[:, hf * T + s_ * 128: hf * T + (s_ + 1) * 128], hT[:, k, s_ * 128:(s_ + 1) * 128], w_[:, k, :],
                           k == 0, k == 7, [wb, B("hT")], [B(name + f"_{hf}")], (k == 7 and s_ == nsub - 1))
                pv = t_[:, hf * T: hf * T + nt].rearrange("p (s c) -> p s c", c=128)
                vd = dsts[key]
                act(vd[:, 0:nsub, 0, 0:64], pv[:, :, 0:64], AF.Copy, [B(name + f"_{hf}")], dsts[key + "_b"])
                act(vd[:, 0:nsub, 1, 64:128], pv[:, :, 64:128], AF.Copy, [B(name + f"_{hf}")], dsts[key + "_b"])

            sl = [0]

            def nxt():
                sl[0] += 1
                return sl[0] % 2
            for i in range(2):
                pb, pn = proj_fm(i)
                act(dsts["a"][:, i, :], pb[:, 0:nt], AF.Copy, [B(pn)], dsts["a_b"])
            for j in range(3):
                pb, pn = proj_fm(2 + j)
                if need_q:
                    qk_post(pb, pn, nt, l, False, 0, latent, dsts["qw"][:, j, :], dsts["qw_b"], nxt())
            pb, pn = proj_fm(5)
            qk_post(pb, pn, nt, l, False, 0, latent, dsts["kw"], dsts["kw_b"], nxt())
            proj_v(6, "vw")
            for j in range(3):
                pb, pn = proj_fm(7 + j)
                if need_q:
                    qk_post(pb, pn, nt, l, True, 0, latent, dsts["qg"][:, j, :], dsts["qg_b"], nxt())
            pb, pn = proj_fm(10)
            qk_post(pb, pn, nt, l, True, 1, latent, dsts["kg"], dsts["kg_b"], nxt())
            proj_v(11, "vg")

        ptc = [0]
        ssc = [0]

        def attend(qsrc, qbufs, nq, tiles, sink_col, ydst, l):
            n = len(tiles)
            sslots = []

            def qk(i):
                kT, kb, _, _, _, pre = tiles[i]
                if pre is not None:
                    pre()
                s_ = ssc[0] % 2
                ssc[0] += 1
                mk_ = tiles[i][4]
                mm(ps_s[s_][:, 0:nq], kT[0:64, :], qsrc[0:64, 0:nq], True, mk_ is None, kb + qbufs, [B(f"ps_s{s_}_0")], False)
                mm(ps_s[s_][:, T:T + nq], kT[64:128, :], qsrc[64:128, 0:nq], True, mk_ is None, kb + qbufs, [B(f"ps_s{s_}_1")], mk_ is None)
                if mk_ is not None:
                    mm(ps_s[s_][:, 0:nq], ident_bf, mk_[:, 0:nq], False, True, [B("cbf"), B("masks")], [B(f"ps_s{s_}_0")], False)
                    mm(ps_s[s_][:, T:T + nq], ident_bf, mk_[:, 0:nq], False, True, [B("cbf"), B("masks")], [B(f"ps_s{s_}_1")], True)
                sslots.append(s_)

            qk(0)
            for i in range(n):
                if i + 1 < n:
                    qk(i + 1)
                s_ = sslots[i]
                p_ = ptc[0] % 3
                ptc[0] += 1
                _, _, vv, vb, mk, _ = tiles[i]
                pt = PT[p_]
                ptb = B(f"PT{p_}")
                if nq == T:
                    act(pt[:, :], ps_s[s_][:, :], AF.Exp, [B(f"ps_s{s_}_0"), B(f"ps_s{s_}_1")], [ptb], scale=0.125)
                else:
                    act(pt[:, :].rearrange("p (h t) -> p h t", h=2)[:, :, 0:nq],
                        ps_s[s_][:, :].rearrange("p (h t) -> p h t", h=2)[:, :, 0:nq], AF.Exp,
                        [B(f"ps_s{s_}_0"), B(f"ps_s{s_}_1")], [ptb], scale=0.125)
                mm(ps_acc[0][:, 0:nq], vv[:, 0, :], pt[:, 0:nq], i == 0, i == n - 1, vb + [ptb], [B("ps_acc0_0")], False)
                mm(ps_acc[1][:, 0:nq], vv[:, 1, :], pt[:, T:T + nq], i == 0, i == n - 1, vb + [ptb], [B("ps_acc1_0")], True)
                yield
            vcopy(dsb[0:64, 0:nq], ps_acc[1][0:64, 0:nq], [B("ps_acc1_0")], [B("dsb")])
            vcopy(dsb[64:128, 0:nq], ps_acc[0][64:128, 0:nq], [B("ps_acc0_0")], [B("dsb")])
            w_ = ssc[0] % 2
            ssc[0] += 1
            wps = ps_s[w_][:, 0:nq]
            wpb = B(f"ps_s{w_}_0")
            mm(wps, swap, dsb[:, 0:nq], True, True, [B("consts"), B("dsb")], [wpb], True)
            yield
            if sink_col is not None:
                tsm(rec[:, 0:nq], wps, esink[:, l, sink_col:sink_col + 1], [wpb, B("esink")], [B("rec")], op=ALU.add)
                recip(rec[:, 0:nq], rec[:, 0:nq], [B("rec")], [B("rec")])
            else:
                recip(rec[:, 0:nq], wps, [wpb], [B("rec")])
            tt(ydst[0:64, 0:nq], ps_acc[0][0:64, 0:nq], rec[0:64, 0:nq], ALU.mult, [B("ps_acc0_0"), B("rec")], [B("yT")])
            tt(ydst[64:128, 0:nq], ps_acc[1][64:128, 0:nq], rec[64:128, 0:nq], ALU.mult, [B("ps_acc1_0"), B("rec")], [B("yT")])
            yield

        def pool_mix(nt, l, edges):
            W = nt + 16
            ab = [B("aext")]
            yb = [B("pwt")]
            for c in range(2):
                a_ = aext[:, c, :]
                w2, w4, w8 = pwt[0], pwt[1], pwt[2]
                tt(w2[:, 1:W], a_[:, 1:W], a_[:, 0:W - 1], ALU.add, ab + yb, yb)
                tt(w4[:, 2:W - 1], w2[:, 3:W], w2[:, 1:W - 2], ALU.add, yb, yb)
                if c == 0:
                    srcs = ((0, 64, w2, 2), (64, 128, w4, 4))
                else:
                    tt(w8[:, 4:W - 3], w4[:, 2:W - 5], w4[:, 6:W - 1], ALU.add, yb, yb)
                    tt(w2[64:128, 8:W - 7], w8[64:128, 4:W - 11], w8[64:128, 12:W - 3], ALU.add, yb, yb)
                    srcs = ((0, 64, w8, 8), (64, 128, w2, 16))
                yield
                for (p0, p1, wsrc, wn) in srcs:
                    stt(feat[p0:p1, c, 0:nt], wsrc[p0:p1, 8:8 + nt], 1.0 / wn, a_[p0:p1, 8:8 + nt], ALU.mult, ALU.subtract,
                        yb + ab, [B("feat")])
                    for (idx, side) in edges:
                        c0 = 0 if side == 0 else nt - 8
                        tt(tmp[2][p0:p1, 0:8], wsrc[p0:p1, 8 + c0:16 + c0], icnt[p0:p1, idx, c, side * 8:side * 8 + 8], ALU.mult,
                           yb + [B("icnt")], [B("tmp2")])
                        tt(feat[p0:p1, c, c0:c0 + 8], tmp[2][p0:p1, 0:8], a_[p0:p1, 8 + c0:16 + c0], ALU.subtract,
                           [B("tmp2")] + ab, [B("feat")])
                w_ = ssc[0] % 2
                ssc[0] += 1
                mm(ps_s[w_][:, 0:nt], wpool[:, c, :], feat[:, c, 0:nt], True, True, [B("wpool"), B("feat")], [B(f"ps_s{w_}_0")], True)
                yield
                tsm(yT[:, c, 0:nt], ps_s[w_][:, 0:nt], pscale[:, l, c:c + 1], [B(f"ps_s{w_}_0"), B("pscale")], [B("yT")])
                yield

        def g_stats(src_ap, src_bufs, nt):
            act(sq[:, :, 0:nt], src_ap, AF.Square, src_bufs, [B("actT")])
            yield
            for k in range(8):
                mm(ps_m[0][:, 0:nt], ones_bf, sq[:, k, 0:nt], k == 0, k == 7, [B("cbf"), B("actT")], [B("ps_m0")], k == 7)
            yield
            act(rstd[:, 0:nt], ps_m[0][:, 0:nt], AF.Ln, [B("ps_m0"), B("epsc")], [B("rstd")], bias=epsc[:, 0:1], scale=1.0 / D)
            act(rstd[:, 0:nt], rstd[:, 0:nt], AF.Exp, [B("rstd")], [B("rstd")], scale=-0.5)
            yield

        def g_post_res(nt, l, gi, v, xbuf, xname):
            yield from g_stats(yo[:, :, 0:nt], [B("yo")], nt)
            for k in range(8):
                t = tmp[k % 2]
                tt(t[:, 0:nt], yo[:, k, 0:nt], rstd[:, 0:nt], ALU.mult, [B("yo"), B("rstd")], [B(f"tmp{k % 2}")])
                stt(xbuf[:, k, 0:nt], t[:, 0:nt], tabs[:, l, gi, k, v:v + 1], xbuf[:, k, 0:nt], ALU.mult, ALU.add,
                    [B(f"tmp{k % 2}"), B("tabs"), B(xname)], [B(xname)])
                if k % 2 == 1:
                    yield

        def g_norm_mod(src, src_bufs, nt, l, ti, v):
            yield from g_stats(src[:, :, 0:nt], src_bufs, nt)
            for k in range(8):
                t = tmp[k % 2]
                stt(t[:, 0:nt], src[:, k, 0:nt], tabs[:, l, ti, k, v:v + 1], rstd[:, 0:nt], ALU.mult, ALU.mult,
                    src_bufs + [B("tabs"), B("rstd")], [B(f"tmp{k % 2}")])
                act(hT[:, k, 0:nt], t[:, 0:nt], AF.Identity, [B(f"tmp{k % 2}"), B("tabs")], [B("hT")],
                    bias=tabs[:, l, ti + 1, k, v:v + 1], scale=1.0)
                if k % 2 == 1:
                    yield

        def g_wout(nt, l, ws):
            for m in range(8):
                o = ps_m[m % 2][:, 0:nt]
                w_, wb = ws.get(m)
                for k in range(8):
                    mm(o, w_[:, k, :], yT[:, k, 0:nt], k == 0, k == 7, [wb, B("yT")], [B(f"ps_m{m % 2}")], k == 7)
                act(yo[:, m, 0:nt], o, AF.Copy, [B(f"ps_m{m % 2}")], [B("yo")])
                yield

        def g_tail(nt, l, v, xbuf, xname, ws):
            yield from g_post_res(nt, l, 2, v, xbuf, xname)
            yield from g_norm_mod(xbuf, [B(xname)], nt, l, 3, v)
            dl = [ldB(wdown_s[l, 0], B(f"wdown_s{l}"))]
            og = ps_m[0][:, 0:nt]
            ou = ps_m[1][:, 0:nt]
            for m in range(NFF):
                (wg, wgb), (wu, wub) = ws.get(8 + 2 * m), ws.get(9 + 2 * m)
                s_ = m % 2
                for k in range(8):
                    mm(og, wg[:, k, :], hT[:, k, 0:nt], k == 0, k == 7, [wgb, B("hT")], [B("ps_m0")], k == 7)
                act(sg[s_][:, 0:nt], og, AF.Exp, [B("ps_m0")], [B(f"sg{s_}")], scale=-1.0)
                yield
                for k in range(8):
                    mm(ou, wu[:, k, :], hT[:, k, 0:nt], k == 0, k == 7, [wub, B("hT")], [B("ps_m1")], k == 7)
                tsm(sg[s_][:, 0:nt], sg[s_][:, 0:nt], 1.0, [B(f"sg{s_}")], [B(f"sg{s_}")], op=ALU.add)
                recip(sg[s_][:, 0:nt], sg[s_][:, 0:nt], [B(f"sg{s_}")], [B(f"sg{s_}")])
                tt(sg[s_][:, 0:nt], og, sg[s_][:, 0:nt], ALU.mult, [B("ps_m0"), B(f"sg{s_}")], [B(f"sg{s_}")])
                tt(actT[:, m, 0:nt], sg[s_][:, 0:nt], ou, ALU.mult, [B(f"sg{s_}"), B("ps_m1")], [B("actT")])
                yield
            dl.append(ldB(wdown_s[l, 1], B(f"wdown_s{l}")))
            for mo in range(8):
                wd, wdb = dl[mo]
                a_ = mo % 2
                o = ps_m[a_][:, 0:nt]
                for k in range(NFF):
                    mm(o, wd[:, k, :], actT[:, k, 0:nt], k == 0, k == NFF - 1, [wdb, B("actT")], [B(f"ps_m{a_}")], k == NFF - 1)
                    if k == 10:
                        yield
                if mo + 2 < 8:
                    dl.append(ldB(wdown_s[l, mo + 2], B(f"wdown_s{l}")))
                act(yo[:, mo, 0:nt], o, AF.Copy, [B(f"ps_m{a_}")], [B("yo")])
                yield
            yield from g_post_res(nt, l, 5, v, xbuf, xname)

        def wstream(l, reps=1):
            one = [(wout_s[l, m], B(f"wout_s{l}")) for m in range(8)] + \
                  [(w[l, m], B(f"{nm}{l}")) for m in range(NFF) for (w, nm) in ((wgate_s, "wgate_s"), (wup_s, "wup_s"))]
            return Stream(one * reps)

        class WView:
            def __init__(self, st_, base):
                self.st_, self.base = st_, base

            def get(self, i):
                return self.st_.get(self.base + i)

        def drain(g):
            for _ in g:
                pass

        def interleave(ga, gt, ratio):
            a_alive, t_alive = ga is not None, gt is not None
            while a_alive or t_alive:
                if a_alive:
                    for _ in range(ratio):
                        try:
                            next(ga)
                        except StopIteration:
                            a_alive = False
                            break
                if t_alive:
                    try:
                        next(gt)
                    except StopIteration:
                        t_alive = False

        for l in range(L):
            last = (l == FL - 1)
            lp = l % 2
            X_in = xT_in if l == 0 else x_s
            X_out = out_T if l == L - 1 else x_s
            xin_b = (lambda c: B("xT_in")) if l == 0 else (lambda c: B(f"x_s{c}"))
            xout_b = (lambda c: B(f"outT{c}")) if l == L - 1 else (lambda c: B(f"x_s{c}"))

            P.dma("pool", wpool[:], wpool_in[l], reads=[], writes=[B("wpool")])

            norm_mod(xc, [B("xc")], CTX, l, 0, 1)
            in_proj(CTX, l, False, dict(
                a=aext[:, :, 8:8 + CTX], a_b=[B("aext")],
                qw=qwc, qw_b=[B("qwc")], kw=kwc[:, :], kw_b=[B("kwc")],
                qg=qgc, qg_b=[B("qgc")], kg=kgc[:, :], kg_b=[B("kgc")],
                vw=vwc, vw_b=[B("vwc")], vg=vgc, vg_b=[B("vgc")]), need_q=not last)
            if not last:
                P.op("dve", lambda: nc.vector.memset(aext[:, :, 0:8], 0.0), writes=[B("aext")])
                P.op("dve", lambda: nc.vector.memset(aext[:, :, 8 + CTX:16 + CTX], 0.0), writes=[B("aext")])
                drain(pool_mix(CTX, l, [(2, 0), (2, 1)]))
                for j in range(3):
                    tiles = [(kwc[:, i * 128:(i + 1) * 128], [B("kwc")], vwc[:, i], [B("vwc")], None, None) for i in range(2)]
                    drain(attend(qwc[:, j, :], [B("qwc")], CTX, tiles, j, yT[:, 2 + j, :], l))
                for j in range(3):
                    tiles = [(kgc[:, i * 128:(i + 1) * 128], [B("kgc")], vgc[:, i], [B("vgc")], None, None) for i in range(2)]
                    drain(attend(qgc[:, j, :], [B("qgc")], CTX, tiles, None, yT[:, 5 + j, :], l))
                wsx = wstream(l)
                drain(g_wout(CTX, l, wsx))
                drain(g_tail(CTX, l, 1, xc, "xc", wsx))

            def load_x(c):
                load(xa[:], X_in[:, :, c * T:(c + 1) * T].rearrange("k p t -> p k t"), [xin_b(c)], [B("xa")])

            load_x(0)
            for c in range(NCH):
                cs = slice(c * T, (c + 1) * T)
                load(ropeC[:], ropeC_in[:, cs], [], [B("ropeC")])
                load(ropeS[:], ropeS_in[:, cs], [], [B("ropeS")])
                norm_mod(xa, [B("xa")], T, l, 0, 0)
                if c + 1 < NCH:
                    load_x(c + 1)
                in_proj(T, l, True, dict(
                    a=ast, a_b=[B("ast")],
                    qw=qst[0], qw_b=[B("qst0")], kw=kst[0][:, :], kw_b=[B("kst0")],
                    qg=qst[1], qg_b=[B("qst1")], kg=kst[1][:, :], kg_b=[B("kst1")],
                    vw=vst[0], vw_b=[B("vst0")], vg=vst[1], vg_b=[B("vst1")]))
                dp = bass.ds(par_p, 1)
                store(a_t[lp][c][dp].rearrange("o p c t -> p (o c) t"), ast[:], [B("ast")], [B("d_a")])
                store(qw_s[lp, :, :, cs], qst[0][:], [B("qst0")], [B("d_qw")])
                store(qg_s[lp, :, :, cs], qst[1][:], [B("qst1")], [B("d_qg")])
                store(kw_t[lp][c][dp].rearrange("o p t -> p o t"), kst[0][:].rearrange("p (o t) -> p o t", o=1), [B("kst0")], [B("d_kw")])
                store(kg_t[lp][c][dp].rearrange("o p t -> p o t"), kst[1][:].rearrange("p (o t) -> p o t", o=1), [B("kst1")], [B("d_kg")])
                store(vw_t[lp][c][dp].rearrange("o s p f -> p (o s) f"),
                      vst[0][:].rearrange("p s k f -> p s (k f)"), [B("vst0")], [B("d_vw")])
                store(vg_t[lp][c][dp].rearrange("o s p f -> p (o s) f"),
                      vst[1][:].rearrange("p s k f -> p s (k f)"), [B("vst1")], [B("d_vg")])

            P.wait_bufs("pool", [B(n) for n in ("d_a", "d_qw", "d_qg", "d_kw", "d_kg", "d_vw", "d_vg")])
            nc.all_core_barrier()
            if l + 1 < L:
                cast_layer(l + 1)

            groups = [(hh, g) for hh in range(2) for g in range(4)]

            def issue_group(hh, g, s_):
                for u in range(2):
                    load(kgs[s_][:, u * T:(u + 1) * T], kg_t[lp][2 * g + u][hh], [], [B(f"kgs{s_}")])
                for u in range(2):
                    load(vgs[s_][:, 4 * u:4 * u + 4].rearrange("p s k f -> p s (k f)"), vg_t[lp][2 * g + u][hh].rearrange("s p f -> p s f"), [], [B(f"vgs{s_}")])

            def g_attn(c):
                cs = slice(c * T, (c + 1) * T)
                load(qst[0][:], qw_s[lp, :, :, cs], [], [B("qst0")])
                load(qst[1][:], qg_s[lp, :, :, cs], [], [B("qst1")])
                own = bass.ds(par, 1)
                oth = bass.ds(1 - par, 1)
                lo_sel, lo_c = (own, c - 1) if c > 0 else (oth, NCH - 1)
                hi_sel, hi_c = (own, c + 1) if c < NCH - 1 else (oth, 0)

                def k3(ap_):
                    return ap_.rearrange("p (o t) -> p o t", o=1)
                load(k3(kwin[:, 0:128]), kw_t[lp][lo_c][lo_sel, :, T - 128:T].rearrange("o p t -> p o t"), [], [B("kwin")])
                load(k3(kwin[:, 128:640]), kw_t[lp][c][own].rearrange("o p t -> p o t"), [], [B("kwin")])
                load(k3(kwin[:, 640:768]), kw_t[lp][hi_c][hi_sel, :, 0:128].rearrange("o p t -> p o t"), [], [B("kwin")])
                vw3 = vwin[:].rearrange("p s k f -> p s (k f)")
                load(vw3[:, 0:1], vw_t[lp][lo_c][lo_sel, 3:4].rearrange("o s p f -> p (o s) f"), [], [B("vwin")])
                load(vw3[:, 1:5], vw_t[lp][c][own].rearrange("o s p f -> p (o s) f"), [], [B("vwin")])
                load(vw3[:, 5:6], vw_t[lp][hi_c][hi_sel, 0:1].rearrange("o s p f -> p (o s) f"), [], [B("vwin")])
                load(aext[:, :, 0:8], a_t[lp][lo_c][lo_sel, :, :, T - 8:T].rearrange("o p c t -> p (o c) t"), [], [B("aext")])
                load(aext[:, :, 8:8 + T], a_t[lp][c][own].rearrange("o p c t -> p (o c) t"), [], [B("aext")])
                load(aext[:, :, 8 + T:16 + T], a_t[lp][hi_c][hi_sel, :, :, 0:8].rearrange("o p c t -> p (o c) t"), [], [B("aext")])
                if c == 0:
                    issue_group(groups[0][0], groups[0][1], 0)
                yield

                def g_pool():
                    edges = []
                    if c == 0:
                        tsm(aext[:, :, 0:8], aext[:, :, 0:8], halo[:, 0:1], [B("aext"), B("halo")], [B("aext")])
                        edges.append((0, 0))
                    if c == NCH - 1:
                        tsm(aext[:, :, 8 + T:16 + T], aext[:, :, 8 + T:16 + T], halo[:, 1:2], [B("aext"), B("halo")], [B("aext")])
                        edges.append((1, 1))
                    yield from pool_mix(T, l, edges)
                for j in range(3):
                    tiles = []
                    for r in range(6):
                        if r == 0:
                            mk = masks[:, 6 if c == 0 else 0, :]
                        elif r == 5:
                            mk = masks[:, 7 if c == NCH - 1 else 5, :]
                        else:
                            mk = masks[:, r, :]
                        tiles.append((kwin[:, r * 128:(r + 1) * 128], [B("kwin")], vwin[:, r], [B("vwin")], mk, None))
                    for i in range(2):
                        tiles.append((kwc[:, i * 128:(i + 1) * 128], [B("kwc")], vwc[:, i], [B("vwc")], None, None))
                    yield from attend(qst[0][:, j, :], [B("qst0")], T, tiles, j, yT[:, 2 + j, :], l)
                for j in range(3):
                    tiles = [(kgc[:, i * 128:(i + 1) * 128], [B("kgc")], vgc[:, i], [B("vgc")], None, None) for i in range(2)]
                    for gi, (hh, g) in enumerate(groups):
                        s_ = gi % 2
                        if gi + 1 < len(groups):
                            nh = groups[gi + 1]
                        elif j + 1 < 3 or c + 1 < NCH:
                            nh = groups[0]
                        else:
                            nh = None
                        for i in range(8):
                            pre = None
                            if i == 2 and nh is not None:
                                pre = (lambda nh=nh, ns=(gi + 1) % 2: issue_group(nh[0], nh[1], ns))
                            tiles.append((kgs[s_][:, i * 128:(i + 1) * 128], [B(f"kgs{s_}")], vgs[s_][:, i], [B(f"vgs{s_}")], None, pre))
                    if j == 2:
                        yield from g_pool()
                    yield from attend(qst[1][:, j, :], [B("qst1")], T, tiles, None, yT[:, 5 + j, :], l)
                return
                load(aext[:, :, 0:8], a_t[lp][lo_c][lo_sel, :, :, T - 8:T].rearrange("o p c t -> p (o c) t"), [], [B("aext")])
                load(aext[:, :, 8:8 + T], a_t[lp][c][own].rearrange("o p c t -> p (o c) t"), [], [B("aext")])
                load(aext[:, :, 8 + T:16 + T], a_t[lp][hi_c][hi_sel, :, :, 0:8].rearrange("o p c t -> p (o c) t"), [], [B("aext")])
                yield
                edges = []
                if c == 0:
                    tsm(aext[:, :, 0:8], aext[:, :, 0:8], halo[:, 0:1], [B("aext"), B("halo")], [B("aext")])
                    edges.append((0, 0))
                if c == NCH - 1:
                    tsm(aext[:, :, 8 + T:16 + T], aext[:, :, 8 + T:16 + T], halo[:, 1:2], [B("aext"), B("halo")], [B("aext")])
                    edges.append((1, 1))
                yield from pool_mix(T, l, edges)

            drain(g_attn(0))
            wsl = wstream(l, NCH)
            for c in range(NCH):
                cs = slice(c * T, (c + 1) * T)
                ws = WView(wsl, c * (8 + 2 * NFF))
                drain(g_wout(T, l, ws))
                ga = g_attn(c + 1) if c + 1 < NCH else None
                if ga is not None:
                    for _ in range(2):
                        next(ga)
                load(xa[:], X_in[:, :, cs].rearrange("k p t -> p k t"), [xin_b(c)], [B("xa")])
                interleave(ga, g_tail(T, l, 0, xa, "xa", ws), RATIO)
                store(X_out[:, :, cs].rearrange("k p t -> p k t"), xa[:], [B("xa")], [xout_b(c)])

        P.wait_bufs("pool", [B(f"outT{c}") for c in range(NCH)])
    return nc


def _pair_cols(base):
    return [np.r_[base + j * 64: base + j * 64 + 64, base + (3 + j) * 64: base + (3 + j) * 64 + 64] for j in range(3)]


def _cnt(t, w, n):
    lo = np.clip(t - w // 2, 0, n - 1)
    hi = np.clip(t + (w - 1 - w // 2), 0, n - 1)
    return (hi - lo + 1).astype(np.float32)


def _prep_inputs(inp, L):
    f32 = np.float32
    sh = {}
    in_chunks = [np.arange(0, 128), np.arange(128, 256)] + _pair_cols(256) + [np.arange(640, 768), np.arange(768, 896)] \
        + _pair_cols(896) + [np.arange(1280, 1408), np.arange(1408, 1536)]
    cols = np.concatenate(in_chunks)
    w = np.asarray(inp["w_in"], f32)[:L][:, :, cols]
    sh["w_in"] = np.ascontiguousarray(w.reshape(L, 8, 128, 12, 128).transpose(0, 3, 2, 1, 4))
    rows = np.concatenate([np.arange(0, 128), np.arange(128, 256)] + _pair_cols(256) + _pair_cols(640))
    w = np.asarray(inp["w_out"], f32)[:L][:, rows, :]
    sh["w_out"] = np.ascontiguousarray(w.reshape(L, 8, 128, 8, 128).transpose(0, 3, 2, 1, 4))
    for k in ("w_gate", "w_up"):
        w = np.asarray(inp[k], f32)[:L]
        sh[k] = np.ascontiguousarray(w.reshape(L, 8, 128, NFF, 128).transpose(0, 3, 2, 1, 4))
    w = np.asarray(inp["w_down"], f32)[:L]
    sh["w_down"] = np.ascontiguousarray(w.reshape(L, NFF, 128, 8, 128).transpose(0, 3, 2, 1, 4))
    w = np.asarray(inp["w_mod"], f32)[:L]
    sh["wmod"] = np.ascontiguousarray(w.reshape(L, 8, 128, 12, 512).transpose(0, 3, 2, 1, 4))
    b = np.asarray(inp["b_mod"], f32)[:L]
    sh["bmod"] = np.ascontiguousarray(np.stack([b, b], axis=1))
    g = np.stack([np.asarray(inp[k], f32)[:L] for k in ("g_pre_mix", "g_post_mix", "g_pre_ffn", "g_post_ffn")], axis=1)
    sh["gains"] = np.ascontiguousarray(g.reshape(L, 4, 8, 128).transpose(3, 0, 1, 2))
    ps = np.asarray(inp["pool_scale"], f32)[:L]
    sh["pscale"] = np.ascontiguousarray(ps.reshape(L, 2, 128).transpose(2, 0, 1))
    ws = np.asarray(inp["win_sink"], f32)[:L]
    sk = np.zeros((128, L, 3), f32)
    sk[0:64] = ws[None, :, 0:3]
    sk[64:128] = ws[None, :, 3:6]
    sh["sink"] = sk
    gq = np.asarray(inp["g_qnorm"], f32)[:L]
    gk = np.asarray(inp["g_knorm"], f32)[:L]
    gqk = np.zeros((128, L, 2), f32)
    gqk[:, :, 0] = np.concatenate([gq, gq], axis=1).T
    gqk[:, :, 1] = np.concatenate([gk, gk], axis=1).T
    sh["gqk"] = gqk
    wp = np.asarray(inp["w_pool"], f32)[:L]
    wpb = np.zeros((L, 128, 2, 128), f32)
    for c in range(2):
        for gl in range(2):
            wpb[:, gl * 64:(gl + 1) * 64, c, gl * 64:(gl + 1) * 64] = wp[:, 2 * c + gl]
    sh["w_pool"] = wpb
    consts = np.zeros((128, 3, 128), f32)
    for m in range(128):
        d = m % 64
        half = (d % 32) // 16
        partner = m + 16 if half == 0 else m - 16
        consts[partner, 0, m] = 1.0
        consts[(m + 64) % 128, 1, m] = 1.0
    consts[0, 2, 0] = 1.0
    consts[1, 2, 1] = 1.0
    sh["consts"] = consts
    cbf = np.zeros((128, 3, 128), f32)
    cbf[:, 0, :] = 1.0
    cbf[0:64, 1, 0:64] = 1.0
    cbf[64:128, 1, 64:128] = 1.0
    cbf[np.arange(128), 2, np.arange(128)] = 1.0
    sh["cbf"] = cbf.astype(ml_dtypes.bfloat16)
    jj = np.arange(128)[:, None]
    qq = np.arange(T)[None, :]
    NEGM = np.float32(-30000.0)
    base_masks = np.stack([np.where(np.abs(128 * (r - 1) + jj - qq) <= 128, np.float32(0.0), NEGM) for r in range(6)], axis=1).astype(f32)
    freq = (np.float32(10000.0) ** (-np.arange(16, dtype=f32) / np.float32(16))).astype(f32)
    x = np.asarray(inp["x"], f32)
    ctx = np.asarray(inp["ctx"], f32)
    cvec = np.asarray(inp["c"], f32)
    cctx = np.asarray(inp["c_ctx"], f32)
    per_core = []
    pwin = (2, 4, 8, 16)
    for r in range(8):
        b_, h_ = r // 2, r % 2
        m = dict(sh)
        xs = x[b_, h_ * HALF:(h_ + 1) * HALF, :]
        m["xT"] = np.ascontiguousarray(xs.T).reshape(8, 128, HALF)
        m["xcT"] = np.ascontiguousarray(ctx[b_].T.reshape(8, 128, CTX).transpose(1, 0, 2))
        cc = np.stack([cvec[b_], cctx], axis=1)
        m["ccT"] = np.ascontiguousarray(cc.reshape(8, 128, 2).transpose(1, 0, 2))
        tg = (h_ * HALF + np.arange(HALF)).astype(f32)
        rowp = np.floor(tg / 64).astype(f32)
        colp = (tg - rowp * 64).astype(f32)
        C = np.zeros((128, HALF), f32)
        S = np.zeros((128, HALF), f32)
        for p in range(128):
            d = p % 64
            axis = d // 32
            half = (d % 32) // 16
            f = d % 16
            ang = ((rowp if axis == 0 else colp) * freq[f]).astype(f32)
            C[p] = np.cos(ang)
            S[p] = np.sin(ang) * (-1.0 if half == 0 else 1.0)
        m["ropeC"] = C
        m["ropeS"] = S
        mk = np.full((128, 8, T), NEGM, f32)
        mk[:, 0:6] = base_masks
        if h_ == 1:
            mk[:, 6] = base_masks[:, 0]
        if h_ == 0:
            mk[:, 7] = base_masks[:, 5]
        m["masks"] = mk.astype(ml_dtypes.bfloat16)
        ic = np.zeros((128, 3, 2, 16), f32)
        for p in range(128):
            for c in range(2):
                w_ = pwin[2 * c + p // 64]
                ic[p, :, c, :] = 1.0 / w_
                t0 = h_ * HALF + np.arange(8)
                ic[p, 0, c, 0:8] = 1.0 / _cnt(t0, w_, SEQ)
                t1 = h_ * HALF + HALF - 8 + np.arange(8)
                ic[p, 1, c, 8:16] = 1.0 / _cnt(t1, w_, SEQ)
                ic[p, 2, c, 0:8] = 1.0 / _cnt(np.arange(8), w_, CTX)
                ic[p, 2, c, 8:16] = 1.0 / _cnt(CTX - 8 + np.arange(8), w_, CTX)
        m["icnt"] = ic
        hl = np.zeros((128, 2), f32)
        hl[:, 0] = 1.0 if h_ == 1 else 0.0
        hl[:, 1] = 1.0 if h_ == 0 else 0.0
        m["halo"] = hl
        per_core.append(m)
    return per_core


_NC_CACHE = {}


def kernel(n_layers=DEPTH, **inputs):
    L = n_layers
    if L not in _NC_CACHE:
        _NC_CACHE[L] = build_program(L)
    nc = _NC_CACHE[L]
    in_maps = _prep_inputs(inputs, L)
    res = run_bass_kernel_spmd(nc, in_maps, core_ids=list(range(8)))
    out = np.empty((BATCH, SEQ, D), np.float32)
    for r in range(8):
        b_, h_ = r // 2, r % 2
        o = np.asarray(res.results[r]["outT"], np.float32).reshape(D, HALF)
        out[b_, h_ * HALF:(h_ + 1) * HALF, :] = o.T
    return out
```
